# Optimizing a Trainium2 kernel written in Bass

```python
import jax, jax.numpy as jnp
from jax import lax
import numpy as np

D_MODEL = 2048
BATCH = 4
SEQ = 4096
DEPTH = 2

N_META = 16
MIX_WIDTH = 1024
N_BRANCH = 3
MLSTM_HEADS = 8
MLSTM_HEAD_DIM = MIX_WIDTH // MLSTM_HEADS
MLSTM_CHUNK = 64
CONV_WIDTH = 3
POOL_WINDOWS = (2, 4, 8, 16)
POOL_GROUP = MIX_WIDTH // len(POOL_WINDOWS)
D_FF = 5632
RMS_EPS = 1e-6
MIXER_SPLITS = (MIX_WIDTH, MIX_WIDTH, MIX_WIDTH, MIX_WIDTH, MLSTM_HEADS, MLSTM_HEADS,
                MIX_WIDTH, MIX_WIDTH, MIX_WIDTH, MIX_WIDTH, N_BRANCH * D_MODEL)
IN_COLS = 8 * MIX_WIDTH + 2 * MLSTM_HEADS + N_BRANCH * D_MODEL

kernel_name = "hybrid_mlstm_shortconv_pool_gated_block"


def rms_norm(x, g):
    xf = x.astype(jnp.float32)
    y = xf * lax.rsqrt(jnp.mean(xf * xf, axis=-1, keepdims=True) + RMS_EPS)
    return (y * g.astype(jnp.float32)).astype(x.dtype)


def causal_dwconv(u, w):
    K = w.shape[0]
    L = u.shape[1]
    up = jnp.pad(u, ((0, 0), (K - 1, 0), (0, 0)))
    y = up[:, 0:L] * w[0]
    for j in range(1, K):
        y = y + up[:, j:j + L] * w[j]
    return y


def mlstm_chunk(state, q, k, v, logi, logf):
    C, n, m = state
    T = q.shape[2]
    b = jnp.cumsum(logf, axis=-1)
    D = b[..., :, None] - b[..., None, :] + logi[..., None, :]
    causal = jnp.tril(jnp.ones((T, T), dtype=bool))
    D = jnp.where(causal, D, -jnp.inf)
    inter = b + m[..., None]
    m_t = jnp.maximum(inter, jnp.max(D, axis=-1))
    inter_w = jnp.exp(inter - m_t)
    s = jnp.einsum('bhtd,bhsd->bhts', q, k) * jnp.exp(D - m_t[..., None])
    num = inter_w[..., None] * jnp.einsum('bhtd,bhde->bhte', q, C) + jnp.einsum('bhts,bhse->bhte', s, v)
    den = inter_w * jnp.einsum('bhtd,bhd->bht', q, n) + jnp.sum(s, axis=-1)
    h = num / jnp.maximum(jnp.abs(den), jnp.exp(-m_t))[..., None]
    b_T = b[..., -1]
    w_log = b_T[..., None] - b + logi
    m_new = jnp.maximum(b_T + m, jnp.max(w_log, axis=-1))
    decay = jnp.exp(b_T + m - m_new)
    ws = jnp.exp(w_log - m_new[..., None])
    C_new = decay[..., None, None] * C + jnp.einsum('bhs,bhsd,bhse->bhde', ws, k, v)
    n_new = decay[..., None] * n + jnp.einsum('bhs,bhsd->bhd', ws, k)
    return (C_new, n_new, m_new), h


def mlstm(q, k, v, i_pre, f_pre, o_pre, head_gain):
    B, L, _ = q.shape
    H, Dh, CH = MLSTM_HEADS, MLSTM_HEAD_DIM, MLSTM_CHUNK
    n_chunks = (L - N_META) // CH

    def heads(t):
        return t.astype(jnp.float32).reshape(B, L, H, Dh).transpose(0, 2, 1, 3)

    qh = heads(q) * (Dh ** -0.5)
    kh = heads(k)
    vh = heads(v)
    logi = i_pre.astype(jnp.float32).transpose(0, 2, 1)
    logf = jax.nn.log_sigmoid(f_pre.astype(jnp.float32)).transpose(0, 2, 1)

    def meta(t):
        return t[:, :, :N_META]

    def chunks(t):
        t = t[:, :, N_META:]
        t = t.reshape((B, H, n_chunks, CH) + t.shape[3:])
        return jnp.moveaxis(t, 2, 0)

    init = (jnp.zeros((B, H, Dh, Dh), jnp.float32), jnp.zeros((B, H, Dh), jnp.float32),
            jnp.zeros((B, H), jnp.float32))
    state, h_meta = mlstm_chunk(init, meta(qh), meta(kh), meta(vh), meta(logi), meta(logf))
    xs = (chunks(qh), chunks(kh), chunks(vh), chunks(logi), chunks(logf))
    _, h_real = lax.scan(lambda st, c: mlstm_chunk(st, *c), state, xs)
    h_real = jnp.moveaxis(h_real, 0, 2).reshape(B, H, n_chunks * CH, Dh)
    h = jnp.concatenate([h_meta, h_real], axis=2)
    h = h * lax.rsqrt(jnp.mean(h * h, axis=-1, keepdims=True) + RMS_EPS)
    h = h.transpose(0, 2, 1, 3).reshape(B, L, H * Dh) * head_gain.astype(jnp.float32)
    return (h * jax.nn.sigmoid(o_pre.astype(jnp.float32))).astype(q.dtype)


def multiscale_pool(u, pool_w, pool_scale):
    B, L, _ = u.shape
    uf = u.astype(jnp.float32)
    cs = jnp.pad(jnp.cumsum(uf, axis=1), ((0, 0), (1, 0), (0, 0)))
    t = jnp.arange(L)
    outs = []
    for g, w in enumerate(POOL_WINDOWS):
        csg = cs[..., g * POOL_GROUP:(g + 1) * POOL_GROUP]
        lagged = jnp.pad(csg, ((0, 0), (w - 1, 0), (0, 0)))[:, :L]
        count = jnp.minimum(t + 1, w).astype(jnp.float32)[None, :, None]
        outs.append((csg[:, 1:] - lagged) / count - uf[..., g * POOL_GROUP:(g + 1) * POOL_GROUP])
    p = jnp.stack(outs, axis=2)
    y = jnp.einsum('blgc,gcd->blgd', p, pool_w.astype(jnp.float32)).reshape(B, L, MIX_WIDTH)
    return (y * pool_scale.astype(jnp.float32)).astype(u.dtype)


def token_mixer(h, w_in, b_if, head_gain, conv_w, pool_w, pool_scale, w_branch, w_out):
    B, L, _ = h.shape
    offsets = [int(o) for o in np.cumsum(MIXER_SPLITS)[:-1]]
    q, k, v, o, ip, fp, cb, cc, cx, pu, gates = jnp.split(h @ w_in, offsets, axis=-1)
    y_a = mlstm(q, k, v, ip + b_if[0], fp + b_if[1], o, head_gain)
    y_b = cb * causal_dwconv(cc * cx, conv_w)
    y_c = multiscale_pool(pu, pool_w, pool_scale)
    ys = jnp.stack([y_a, y_b, y_c], axis=2)
    branch = jnp.einsum('blnc,ncd->blnd', ys, w_branch)
    g = jax.nn.sigmoid(gates.reshape(B, L, N_BRANCH, D_MODEL))
    merged = jnp.sum(g * branch, axis=2)
    return merged @ w_out


def conv_ffn(h, w_ffn_in, conv_w, w_ffn_out):
    a, u = jnp.split(h @ w_ffn_in, 2, axis=-1)
    a = causal_dwconv(a, conv_w)
    return (jax.nn.gelu(a, approximate=True) * u) @ w_ffn_out


def setup_inputs(seed: int = 0) -> dict:
    key = jax.random.key(seed)
    ks = jax.random.split(key, 20)
    nrm = jax.random.normal
    f_bias = jnp.linspace(3.0, 6.0, MLSTM_HEADS)[None, :] + 0.1 * nrm(ks[8], (DEPTH, MLSTM_HEADS))
    i_bias = 0.1 * nrm(ks[7], (DEPTH, MLSTM_HEADS))
    return {
        "x": nrm(ks[0], (BATCH, SEQ, D_MODEL)),
        "meta_tokens": nrm(ks[1], (N_META, D_MODEL)),
        "norm_pre_mix": 1.0 + 0.05 * nrm(ks[2], (DEPTH, D_MODEL)),
        "norm_post_mix": 1.0 + 0.05 * nrm(ks[3], (DEPTH, D_MODEL)),
        "norm_pre_ffn": 1.0 + 0.05 * nrm(ks[4], (DEPTH, D_MODEL)),
        "norm_post_ffn": 1.0 + 0.05 * nrm(ks[5], (DEPTH, D_MODEL)),
        "w_in": nrm(ks[6], (DEPTH, D_MODEL, IN_COLS)) * D_MODEL ** -0.5,
        "b_if": jnp.stack([i_bias, f_bias], axis=1),
        "mlstm_head_gain": 1.0 + 0.05 * nrm(ks[9], (DEPTH, MIX_WIDTH)),
        "conv_mix_w": nrm(ks[10], (DEPTH, CONV_WIDTH, MIX_WIDTH)) * CONV_WIDTH ** -0.5,
        "pool_w": nrm(ks[11], (DEPTH, len(POOL_WINDOWS), POOL_GROUP, POOL_GROUP)) * POOL_GROUP ** -0.5,
        "pool_scale": 1.0 + 0.1 * nrm(ks[12], (DEPTH, MIX_WIDTH)),
        "w_branch": nrm(ks[13], (DEPTH, N_BRANCH, MIX_WIDTH, D_MODEL)) * MIX_WIDTH ** -0.5,
        "w_out": nrm(ks[14], (DEPTH, D_MODEL, D_MODEL)) * D_MODEL ** -0.5,
        "w_ffn_in": nrm(ks[15], (DEPTH, D_MODEL, 2 * D_FF)) * D_MODEL ** -0.5,
        "ffn_conv_w": nrm(ks[16], (DEPTH, CONV_WIDTH, D_FF)) * CONV_WIDTH ** -0.5,
        "w_ffn_out": nrm(ks[17], (DEPTH, D_FF, D_MODEL)) * D_FF ** -0.5,
    }


def reference(x, meta_tokens, norm_pre_mix, norm_post_mix, norm_pre_ffn, norm_post_ffn, w_in, b_if,
              mlstm_head_gain, conv_mix_w, pool_w, pool_scale, w_branch, w_out, w_ffn_in, ffn_conv_w,
              w_ffn_out):
    B = x.shape[0]
    meta = jnp.broadcast_to(meta_tokens.astype(x.dtype)[None], (B, N_META, D_MODEL))
    x = jnp.concatenate([meta, x], axis=1)
    for l in range(DEPTH):
        h = rms_norm(x, norm_pre_mix[l])
        mix = token_mixer(h, w_in[l], b_if[l], mlstm_head_gain[l], conv_mix_w[l], pool_w[l],
                          pool_scale[l], w_branch[l], w_out[l])
        x = x + rms_norm(mix, norm_post_mix[l])
        h = rms_norm(x, norm_pre_ffn[l])
        x = x + rms_norm(conv_ffn(h, w_ffn_in[l], ffn_conv_w[l], w_ffn_out[l]), norm_post_ffn[l])
    return x[:, N_META:]
```

```python
import math
from contextlib import ExitStack

import numpy as np
import concourse.bass as bass
import concourse.mybir as mybir
from concourse.bass_utils import run_bass_kernel_spmd

F32 = mybir.dt.float32
BF16 = mybir.dt.bfloat16
AF = mybir.ActivationFunctionType
ALU = mybir.AluOpType

N_META = 16
MIXW = 1024
NH = 8
DH = 128
CH = 64
EPS = 1e-6
POOL_W = (2, 4, 8, 16)
EPOCH = 24000
ENGS = ("pe", "act", "dve", "pool", "sp")


class Tok:
    __slots__ = ("kind", "key", "pos", "sem", "val", "used")

    def __init__(self, kind, key, pos):
        self.kind, self.key, self.pos = kind, key, pos
        self.sem = None
        self.val = None
        self.used = False


class TokSet:
    __slots__ = ("d",)

    def __init__(self):
        self.d = {}

    def add(self, t):
        o = self.d.get(t.key)
        if o is None or o.pos < t.pos:
            self.d[t.key] = t

    def update(self, other):
        for t in other.d.values():
            self.add(t)

    def toks(self):
        return list(self.d.values())


class Buf:
    __slots__ = ("w", "r")

    def __init__(self):
        self.w = TokSet()
        self.r = TokSet()


class Prog:
    def __init__(self, nc, es):
        self.nc, self.es = nc, es
        self.streams = {e: [] for e in ENGS}
        self.dma_cnt = {}
        self.dry = False
        self.nwaits = 0

    def _deps(self, reads, writes, deps):
        s = TokSet()
        for t in deps:
            if t is not None:
                s.add(t)
        for b in reads:
            s.update(b.w)
        for b in writes:
            s.update(b.w)
            s.update(b.r)
        return s.toks()

    def _post(self, tok, reads, writes, partial):
        for b in reads:
            b.r.add(tok)
        for b in writes:
            if partial:
                b.w.add(tok)
            else:
                b.w = TokSet()
                b.w.add(tok)
                b.r = TokSet()

    def op(self, eng, fn, reads=(), writes=(), deps=(), partial=False):
        st = self.streams[eng]
        tok = Tok("e", ("e", eng), len(st))
        d = self._deps(reads, writes, deps)
        for t in d:
            t.used = True
        st.append((fn, d, tok))
        self._post(tok, reads, writes, partial)
        return tok

    def dma(self, q, out, in_, key, reads=(), writes=(), deps=(), partial=False):
        st = self.streams[q]
        n = self.dma_cnt.get(key, 0) + 1
        self.dma_cnt[key] = n
        tok = Tok("d", ("d", key), n)
        tok.val = 16 * n
        d = self._deps(reads, writes, deps)
        for t in d:
            t.used = True

        def fn(e, out=out, in_=in_):
            return e.dma_start(out=out, in_=in_)

        st.append((fn, d, tok))
        self._post(tok, reads, writes, partial)
        return tok

    def coll(self, src, dst, groups, reads=(), writes=()):
        st = self.streams["pool"]
        self.ncoll = getattr(self, "ncoll", 0) + 1
        tok = Tok("c", ("c", self.ncoll), 1)
        tok.val = 1
        d = self._deps(reads, writes, ())
        for t in d:
            t.used = True

        def fn(e, src=src, dst=dst, groups=groups):
            return e.collective_compute("AllGather", ALU.bypass, replica_groups=groups, ins=[src], outs=[dst])

        st.append((fn, d, tok))
        self._post(tok, reads, writes, False)
        return tok

    def emit(self):
        nc, es = self.nc, self.es
        esems = {}
        for e in ENGS:
            cnt = 0
            for (_, _, tok) in self.streams[e]:
                if tok.kind == "e" and tok.used:
                    ep, v = divmod(cnt, EPOCH)
                    cnt += 1
                    k = (e, ep)
                    if k not in esems:
                        esems[k] = es.enter_context(nc.semaphore(f"s_{e}_{ep}"))
                    tok.sem, tok.val = esems[k], v + 1
        dsems = {}
        for key in self.dma_cnt:
            assert 16 * self.dma_cnt[key] < 60000, (key, self.dma_cnt[key])
            dsems[key] = es.enter_context(nc.semaphore(f"d_{len(dsems)}"))
        for e in ENGS:
            for (_, _, tok) in self.streams[e]:
                if tok.kind == "d":
                    tok.sem = dsems[tok.key[1]]
                elif tok.kind == "c":
                    tok.sem = es.enter_context(nc.semaphore(f"c_{tok.key[1]}"))
        block = es.enter_context(nc.Block())
        prog = self

        def run(e, name):
            waited = {}
            for (fn, deps, tok) in prog.streams[name]:
                for t in deps:
                    sid = id(t.sem)
                    if waited.get(sid, 0) >= t.val:
                        continue
                    e.wait_ge(t.sem, t.val)
                    prog.nwaits += 1
                    waited[sid] = t.val
                ins = fn(e)
                if tok.kind == "d":
                    ins.then_inc(tok.sem, 16)
                elif tok.kind == "c":
                    ins.then_inc(tok.sem, 1)
                elif tok.used:
                    ins.then_inc(tok.sem, 1)

        @block.gpsimd
        def _(e):
            run(e, "pool")

        @block.sync
        def _(e):
            run(e, "sp")

        @block.scalar
        def _(e):
            run(e, "act")

        @block.vector
        def _(e):
            run(e, "dve")

        @block.tensor
        def _(e):
            run(e, "pe")


def even_split(n, maxw=512):
    k = -(-n // maxw)
    assert n % 2 == 0
    h = n // 2
    base, rem = divmod(h, k)
    sizes = [2 * (base + (1 if i < rem else 0)) for i in range(k)]
    out, o = [], 0
    for s in sizes:
        out.append((o, s))
        o += s
    return out


class Cfg:
    def __init__(self, D=2048, DFF=5632, NT=2048, NSUP=1, DEPTH=2, stop=99, n_cores=8):
        self.stop = stop
        self.n_cores = n_cores
        assert NSUP == 1
        self.D, self.DFF, self.NT, self.NSUP, self.DEPTH = D, DFF, NT, NSUP, DEPTH
        self.KC = D // 128
        self.FC = DFF // 128
        self.TC = N_META + NT
        self.TS = N_META + NSUP * NT
        self.INC = 8 * MIXW + 2 * NH + 3 * D
        self.NCH = 1 + NT // CH
        o = 0
        self.v_pre_mix = o; o += self.KC
        self.v_post_mix = o; o += self.KC
        self.v_pre_ffn = o; o += self.KC
        self.v_post_ffn = o; o += self.KC
        self.v_gain = o; o += 8
        self.v_convw = o; o += 24
        self.v_pscale = o; o += 8
        self.v_fconv = o; o += 3 * self.FC
        self.NV = o


def build_program(cfg):
    D, DFF, NT, NSUP, DEPTH = cfg.D, cfg.DFF, cfg.NT, cfg.NSUP, cfg.DEPTH
    KC, FC, TC, TS, NCH = cfg.KC, cfg.FC, cfg.TC, cfg.TS, cfg.NCH
    nc = bass.Bass("TRN2", target_bir_lowering=False)
    es = ExitStack()
    P = Prog(nc, es)
    GROUPS = [[2 * i, 2 * i + 1] for i in range(cfg.n_cores // 2)]

    def din(name, shape, dt=F32):
        return nc.dram_tensor(name, list(shape), dt, kind="ExternalInput").ap()

    xin = din("xin", [TS, D])
    vecs = din("vecs", [128, DEPTH * cfg.NV])
    bif = din("bif", [64, DEPTH * 16])
    invc = din("invc", [128, 4 * 16])
    w_in = din("w_in", [DEPTH, D, cfg.INC])
    pool_w = din("pool_w", [DEPTH, 1024, 256])
    w_branch = din("w_branch", [DEPTH, 3 * MIXW, D])
    w_out = din("w_out", [DEPTH, D, D])
    w_ffn_in = din("w_ffn_in", [DEPTH, D, 2 * DFF])
    w_ffn_out = din("w_ffn_out", [DEPTH, DFF, D])
    out = nc.dram_tensor("out", [TS - N_META, D], F32, kind="ExternalOutput").ap()
    ik = "ExternalOutput" if getattr(cfg, "debug", False) else "Internal"
    dbg = nc.dram_tensor("dbg", [128, 8192], F32, kind=ik).ap()
    dbg2 = nc.dram_tensor("dbg2", [128, 8192], BF16, kind=ik).ap()
    xT = nc.dram_tensor("xT", [D, TS], F32, kind=ik).ap()
    oT = nc.dram_tensor("oT", [D, TS], F32, kind=ik).ap()
    ysc = nc.dram_tensor("ysc", [3 * MIXW, TS], BF16, kind=ik).ap()
    gsc = nc.dram_tensor("gsc", [3 * D, TS], BF16, kind=ik).ap()
    ksc = nc.dram_tensor("ksc", [NH, 128, TC], BF16, kind="Internal").ap()
    ktsc = nc.dram_tensor("ktsc", [NH, 64, NCH, 128], BF16, kind="Internal").ap()
    vpsc = nc.dram_tensor("vpsc", [NH, 64, NCH, 130], BF16, kind="Internal").ap()
    send_c = nc.dram_tensor("send_c", [NH * 128, 130], F32, kind="Internal").ap()
    recv_c = nc.dram_tensor("recv_c", [2 * NH * 128, 130], F32, kind="Internal").ap()
    send_x = nc.dram_tensor("send_x", [D, 16], F32, kind="Internal").ap()
    recv_x = nc.dram_tensor("recv_x", [2 * D, 16], F32, kind="Internal").ap()
    flags = din("flags", [128, 2])

    NSLOT = 6
    hTt = es.enter_context(nc.sbuf_tensor("hT", [128, KC, TC], BF16))
    wring = es.enter_context(nc.sbuf_tensor("wring", [128, NSLOT, 2048], BF16))
    vec_t = es.enter_context(nc.sbuf_tensor("vec_t", [128, DEPTH * cfg.NV], F32))
    bif_t = es.enter_context(nc.sbuf_tensor("bif_t", [64, DEPTH * 16], F32))
    invc_t = es.enter_context(nc.sbuf_tensor("invc_t", [128, 64], F32))
    idf = es.enter_context(nc.sbuf_tensor("idf", [128, 128], F32))
    idb = es.enter_context(nc.sbuf_tensor("idb", [128, 128], BF16))
    onesb = es.enter_context(nc.sbuf_tensor("onesb", [128, 128], BF16))
    onesf = es.enter_context(nc.sbuf_tensor("onesf", [64, 128], F32))
    maskf = es.enter_context(nc.sbuf_tensor("maskf", [64, 64], F32))
    Zcar = es.enter_context(nc.sbuf_tensor("Zcar", [128, 8, 130], F32))
    Ccar = es.enter_context(nc.sbuf_tensor("Ccar", [128, 8, 130], BF16))
    dcar = es.enter_context(nc.sbuf_tensor("dcar", [128, 8], F32))
    acar = es.enter_context(nc.sbuf_tensor("acar", [128, FC, 2], F32))
    Cfin = es.enter_context(nc.sbuf_tensor("Cfin", [128, 8, 130], F32))
    flag_t = es.enter_context(nc.sbuf_tensor("flag_t", [128, 2], F32))
    gdummy = es.enter_context(nc.sbuf_tensor("gdummy", [128, 2], F32))
    AW = min(27500, (nc.sbuf_bytes_remaining - 1024) // 4)
    arena = es.enter_context(nc.sbuf_tensor("arena", [128, AW], F32))
    psb = [es.enter_context(nc.psum_tensor(f"psb{i}", [128, 512], F32)) for i in range(8)]
    psbuf = [Buf() for _ in range(8)]

    class Arena:
        def __init__(self):
            self.off = 0

        def reset(self):
            self.off = 0

        def f32(self, shape):
            shape = list(shape[1:]) if shape[0] == 128 else list(shape)
            n = int(np.prod(shape))
            v = arena[:, self.off:self.off + n]
            self.off += n
            assert self.off <= AW, (self.off, AW)
            if len(shape) == 2:
                return v.rearrange("p (a b) -> p a b", b=shape[1])
            return v

        def bf16(self, shape):
            shape = list(shape[1:]) if shape[0] == 128 else list(shape)
            n = int(np.prod(shape))
            nw = (n + 1) // 2
            v = arena[:, self.off:self.off + nw].bitcast(BF16)[:, 0:n]
            self.off += nw
            assert self.off <= AW, (self.off, AW)
            if len(shape) == 2:
                return v.rearrange("p (a b) -> p a b", b=shape[1])
            return v

    AR = Arena()
    LIVE = []
    guard = {"tok": None}
    gd_buf = Buf()

    def guard_now(keep=()):
        bufs = [b for b in LIVE if b not in keep]
        del LIVE[:]
        tok = P.op("dve", lambda e: e.memset(gdummy[:, 0:1], 0.0), writes=bufs + [gd_buf])
        guard["tok"] = tok
        LIVE.extend(keep)

    def phase_begin():
        guard_now()
        AR.reset()

    def NB():
        b = Buf()
        if guard["tok"] is not None:
            b.w.add(guard["tok"])
        LIVE.append(b)
        return b

    cbuf = Buf()

    def c_load(dst, src, key):
        P.dma("sp", dst, src, key=key, writes=[cbuf], partial=True)

    c_load(vec_t[:], vecs, "c0")
    c_load(bif_t[:], bif, "c1")
    c_load(invc_t[:], invc, "c2")
    c_load(flag_t[:], flags, "c3")
    P.op("pool", lambda e: e.memset(Cfin[:], 0.0), writes=[cbuf], partial=True)
    P.op("pool", lambda e: e.memset(idf[:], 0.0), writes=[cbuf], partial=True)
    P.op("pool", lambda e: e.affine_select(out=idf[:], in_=idf[:], pattern=[[-1, 128]], compare_op=ALU.not_equal,
                                           fill=1.0, base=0, channel_multiplier=1), writes=[cbuf], partial=True)
    P.op("pool", lambda e: e.memset(onesf[:], 1.0), writes=[cbuf], partial=True)
    P.op("pool", lambda e: e.memset(maskf[:], 1.0), writes=[cbuf], partial=True)
    P.op("pool", lambda e: e.affine_select(out=maskf[:], in_=maskf[:], pattern=[[1, 64]], compare_op=ALU.is_ge,
                                           fill=0.0, base=0, channel_multiplier=-1), writes=[cbuf], partial=True)
    P.op("pool", lambda e: e.memset(onesb[:], 1.0), writes=[cbuf], partial=True)
    P.op("dve", lambda e: e.tensor_copy(out=idb[:], in_=idf[:]), reads=[cbuf], writes=[cbuf], partial=True)

    def vcol(l, off, i):
        c = l * cfg.NV + off + i
        return vec_t[:, c:c + 1]

    class WStream:
        def __init__(self):
            self.specs = []
            self.i = 0
            self.issued = 0
            self.slots = [Buf() for _ in range(NSLOT)]

        def _issue(self, k):
            src, nkc, ncols = self.specs[k]
            sl = k % NSLOT
            dst = wring[:, sl, 0:nkc * ncols].rearrange("p (k c) -> p k c", c=ncols)
            P.dma("pool", dst, src, key=f"w{sl}", writes=[self.slots[sl]])

        def get(self, src, nkc, ncols):
            assert nkc * ncols <= 2048
            k = self.i
            self.i += 1
            sl = k % NSLOT
            if P.dry:
                self.specs.append((src.rearrange("(k p) c -> p k c", p=128), nkc, ncols))
                return wring[:, sl, 0:nkc * ncols].rearrange("p (k c) -> p k c", c=ncols), self.slots[sl]
            while self.issued < min(len(self.specs), k + NSLOT - 2):
                self._issue(self.issued)
                self.issued += 1
            sl = k % NSLOT
            return wring[:, sl, 0:nkc * ncols].rearrange("p (k c) -> p k c", c=ncols), self.slots[sl]

    W = WStream()

    ring = {"i": 0}

    def next_bank():
        b = ring["i"] % 4
        ring["i"] += 1
        return b

    hT_buf = Buf()

    def mm_group(bank, ncols, pairs, reads, m=128):
        n = len(pairs)

        def fn(e, pairs=pairs, bank=bank, ncols=ncols, m=m):
            ins = None
            for i, (l, r) in enumerate(pairs):
                ins = e.matmul(psb[bank][0:m, 0:ncols], lhsT=l, rhs=r, start=(i == 0), stop=(i == n - 1))
            return ins

        return P.op("pe", fn, reads=reads, writes=[psbuf[bank]])

    def transpose_in():
        phase_begin()
        xt = [AR.f32([128, D]) for _ in range(2)]
        xo = [AR.f32([KC, 128]) for _ in range(2)]
        xtb = [NB(), NB()]
        xob = [NB(), NB()]
        tiles = [(0, N_META)] + [(N_META + 128 * i, 128) for i in range((TS - N_META) // 128)]
        for ti, (t0, nt) in enumerate(tiles):
            sl = ti % 2
            P.dma("sp", xt[sl][0:nt, :], xin[t0:t0 + nt, :], key=f"ti{sl}", writes=[xtb[sl]], reads=[])
            for k0 in range(0, KC, 4):
                b = next_bank()
                kn = min(4, KC - k0)

                def fn(e, sl=sl, k0=k0, kn=kn, nt=nt, b=b):
                    ins = None
                    for k in range(kn):
                        ins = e.matmul(psb[b][:, k * 128:k * 128 + nt], lhsT=xt[sl][0:nt, (k0 + k) * 128:(k0 + k + 1) * 128],
                                       rhs=idf[0:nt, 0:nt], start=True, stop=True)
                    return ins

                P.op("pe", fn, reads=[xtb[sl], cbuf], writes=[psbuf[b]])
                P.op("act", lambda e, sl=sl, k0=k0, kn=kn, nt=nt, b=b: e.activation(
                    out=xo[sl][:, k0:k0 + kn, 0:nt], in_=psb[b][:, 0:kn * 128].rearrange("p (k c) -> p k c", c=128)[:, :, 0:nt],
                    func=AF.Copy), reads=[psbuf[b]], writes=[xob[sl]], partial=(k0 > 0))
            P.dma("sp", xT.rearrange("(k p) t -> p k t", p=128)[:, :, t0:t0 + nt], xo[sl][:, :, 0:nt], key=f"to{sl}",
                  reads=[xob[sl]], writes=[xT_buf], partial=True)

    def transpose_out():
        phase_begin()
        xi = [AR.f32([KC, 128]) for _ in range(2)]
        xo = [AR.f32([128, D]) for _ in range(2)]
        xib = [NB(), NB()]
        xob = [NB(), NB()]
        ntile = (TS - N_META) // 128
        toks = []
        for ti in range(ntile):
            t0 = N_META + 128 * ti
            sl = ti % 2
            P.dma("sp", xi[sl][:], xT.rearrange("(k p) t -> p k t", p=128)[:, :, t0:t0 + 128], key=f"ti{sl}",
                  reads=[xT_buf], writes=[xib[sl]])
            for k0 in range(0, KC, 4):
                b = next_bank()
                kn = min(4, KC - k0)

                def fn(e, sl=sl, k0=k0, kn=kn, b=b):
                    ins = None
                    for k in range(kn):
                        ins = e.matmul(psb[b][:, k * 128:(k + 1) * 128], lhsT=xi[sl][:, k0 + k, :], rhs=idf[:],
                                       start=True, stop=True)
                    return ins

                P.op("pe", fn, reads=[xib[sl], cbuf], writes=[psbuf[b]])
                P.op("act", lambda e, sl=sl, k0=k0, kn=kn, b=b: e.activation(
                    out=xo[sl][:, k0 * 128:(k0 + kn) * 128], in_=psb[b][:, 0:kn * 128], func=AF.Copy),
                    reads=[psbuf[b]], writes=[xob[sl]], partial=(k0 > 0))
            toks.append(P.dma("sp", out[128 * ti:128 * ti + 128, :], xo[sl][:], key=f"to{sl}", reads=[xob[sl]]))
        P.op("sp", lambda e: e.nop(), deps=toks[-2:])

    xT_buf = Buf()
    ksc_buf = Buf(); cfin_buf = Buf(); sendc_buf = Buf(); recvc_buf = Buf(); sendx_buf = Buf(); recvx_buf = Buf()
    oT_buf = Buf()
    ysc_buf = Buf()
    gsc_buf = Buf()

    def sumsq_rstd(src3, ncols, rstd, sq, sqb, rsb, srcbuf):
        P.op("act", lambda e: e.activation(out=sq[:, :, 0:ncols], in_=src3[:, :, 0:ncols], func=AF.Square),
             reads=[srcbuf], writes=[sqb])
        b = next_bank()
        mm_group(b, ncols, [(onesb[:], sq[:, k, 0:ncols]) for k in range(KC)], reads=[sqb, cbuf])
        P.op("act", lambda e: e.activation(out=rstd[:, 0:ncols], in_=psb[b][:, 0:ncols], func=AF.Ln, scale=1.0 / D,
                                           bias=EPS), reads=[psbuf[b]], writes=[rsb])
        P.op("act", lambda e: e.activation(out=rstd[:, 0:ncols], in_=rstd[:, 0:ncols], func=AF.Exp, scale=-0.5),
             reads=[rsb], writes=[rsb])

    def norm_pass(l, voff, g0, lo, n):
        phase_begin()
        PW = 344
        xp = [AR.f32([KC, PW]) for _ in range(2)]
        sq = AR.bf16([KC, PW])
        rstd = AR.f32([128, PW])
        xpb = [NB(), NB()]
        sqb, rsb = NB(), NB()
        pcs = even_split(n, PW)
        for pi, (c0, w) in enumerate(pcs):
            sl = pi % 2
            P.dma("sp", xp[sl][:, :, 0:w], xT.rearrange("(k p) t -> p k t", p=128)[:, :, g0 + lo + c0:g0 + lo + c0 + w],
                  key=f"np{sl}", reads=[xT_buf], writes=[xpb[sl]])
            sumsq_rstd(xp[sl], w, rstd, sq, sqb, rsb, xpb[sl])
            for k in range(KC):
                P.op("dve", lambda e, k=k, sl=sl, w=w, c0=c0: e.scalar_tensor_tensor(
                    out=hTt[:, k, lo + c0:lo + c0 + w], in0=xp[sl][:, k, 0:w], scalar=vcol(l, voff, k), in1=rstd[:, 0:w],
                    op0=ALU.mult, op1=ALU.mult), reads=[xpb[sl], rsb, cbuf], writes=[hT_buf], partial=True)

    def update_pass(l, voff, g0, lo, n, nxt=None):
        phase_begin()
        PW = 344
        xp = [AR.f32([KC, PW]) for _ in range(2)]
        op_ = [AR.f32([KC, PW]) for _ in range(2)]
        sq = AR.bf16([KC, PW])
        rstd = AR.f32([128, PW])
        rstd2 = AR.f32([128, PW])
        xpb = [NB(), NB()]
        opb = [NB(), NB()]
        sqb, rsb, rsb2 = NB(), NB(), NB()
        pcs = even_split(n, PW)
        xv = xT.rearrange("(k p) t -> p k t", p=128)
        ov = oT.rearrange("(k p) t -> p k t", p=128)

        def loads(pi):
            c0, w = pcs[pi]
            sl = pi % 2
            a, b_ = g0 + lo + c0, g0 + lo + c0 + w
            P.dma("sp", op_[sl][:, :, 0:w], ov[:, :, a:b_], key=f"up{sl}", reads=[oT_buf], writes=[opb[sl]])
            P.dma("sp", xp[sl][:, :, 0:w], xv[:, :, a:b_], key=f"np{sl}", reads=[xT_buf], writes=[xpb[sl]])

        loads(0)
        for pi, (c0, w) in enumerate(pcs):
            sl = pi % 2
            a, b_ = g0 + lo + c0, g0 + lo + c0 + w
            if pi + 1 < len(pcs):
                loads(pi + 1)
            sumsq_rstd(op_[sl], w, rstd, sq, sqb, rsb, opb[sl])
            for k in range(KC):
                P.op("dve", lambda e, k=k, w=w, sl=sl: e.scalar_tensor_tensor(
                    out=op_[sl][:, k, 0:w], in0=op_[sl][:, k, 0:w], scalar=vcol(l, voff, k), in1=rstd[:, 0:w],
                    op0=ALU.mult, op1=ALU.mult), reads=[rsb, cbuf], writes=[opb[sl]], partial=True)
                P.op("pool", lambda e, k=k, w=w, sl=sl: e.tensor_tensor(
                    out=xp[sl][:, k, 0:w], in0=xp[sl][:, k, 0:w], in1=op_[sl][:, k, 0:w], op=ALU.add),
                    reads=[opb[sl]], writes=[xpb[sl]], partial=True)
            P.dma("sp", xv[:, :, a:b_], xp[sl][:, :, 0:w], key=f"ux{sl}", reads=[xpb[sl]], writes=[xT_buf], partial=True)
            if nxt is not None:
                l2, voff2 = nxt
                sumsq_rstd(xp[sl], w, rstd2, sq, sqb, rsb2, xpb[sl])
                for k in range(KC):
                    P.op("dve", lambda e, k=k, sl=sl, w=w, c0=c0: e.scalar_tensor_tensor(
                        out=hTt[:, k, lo + c0:lo + c0 + w], in0=xp[sl][:, k, 0:w], scalar=vcol(l2, voff2, k),
                        in1=rstd2[:, 0:w], op0=ALU.mult, op1=ALU.mult), reads=[xpb[sl], rsb2, cbuf], writes=[hT_buf],
                        partial=True)

    def mixer_p1(l, s):
        g0 = s * NT
        lo = 0 if s == 0 else N_META
        phase_begin()
        pcs = even_split(TC, 512)
        chunks = [(0, 0, N_META)] + [(c, N_META + CH * (c - 1), CH) for c in range(1, NCH)]
        act_chunks = chunks if s == 0 else chunks[1:]
        wl = w_in[l]
        Yst = [AR.bf16([128, TC]) for _ in range(2)]; Ystb = [NB(), NB()]
        mark = AR.off
        Gall = AR.f32([NCH, 16]); LF = AR.f32([NCH, 8]); Wall = AR.f32([NCH, 8]); EB = AR.f32([NCH, 8])
        DEC = AR.f32([NCH, 8])
        tmp8 = [AR.f32([128, 8]) for _ in range(2)]
        gb = NB()
        mark2 = AR.off

        wg, wgb = W.get(wl[:, 4 * MIXW:4 * MIXW + 16], KC, 16)
        for (c, c0, nt) in act_chunks:
            def fn(e, c0=c0, nt=nt):
                ins = None
                for k in range(KC):
                    ins = e.matmul(psb[6][0:nt, 0:16], lhsT=hTt[:, k, c0:c0 + nt], rhs=wg[:, k, :], start=(k == 0),
                                   stop=(k == KC - 1))
                return ins

            P.op("pe", fn, reads=[hT_buf, wgb], writes=[psbuf[6]])
            P.op("dve", lambda e, c=c, nt=nt: e.tensor_tensor(out=Gall[0:nt, c, :], in0=psb[6][0:nt, 0:16],
                                                              in1=bif_t[0:nt, l * 16:l * 16 + 16], op=ALU.add),
                 reads=[psbuf[6], cbuf], writes=[gb], partial=True)
            P.op("act", lambda e, c=c, nt=nt: e.activation(out=tmp8[0][0:nt, :], in_=Gall[0:nt, c, 8:16], func=AF.Exp,
                                                           scale=-1.0), reads=[gb], writes=[gb], partial=True)
            P.op("act", lambda e, c=c, nt=nt: e.activation(out=LF[0:nt, c, :], in_=tmp8[0][0:nt, :], func=AF.Ln,
                                                           bias=1.0), reads=[gb], writes=[gb], partial=True)
            if c == 0:
                P.op("dve", lambda e, nt=nt: e.tensor_scalar(out=LF[0:nt, 0, :], in0=LF[0:nt, 0, :],
                                                            scalar1=flag_t[0:nt, 0:1], scalar2=None, op0=ALU.mult),
                     reads=[gb, cbuf], writes=[gb], partial=True)
            P.op("pe", lambda e, c=c, nt=nt: e.matmul(psb[7][0:nt, 0:8], lhsT=maskf[0:nt, 0:nt], rhs=LF[0:nt, c, :],
                                                      start=True, stop=True), reads=[gb, cbuf], writes=[psbuf[7]])
            P.op("pe", lambda e, c=c, nt=nt: e.matmul(psb[5][:, 0:8], lhsT=onesf[0:nt, :], rhs=LF[0:nt, c, :],
                                                      start=True, stop=True), reads=[gb, cbuf], writes=[psbuf[5]])
            P.op("act", lambda e, c=c: e.activation(out=DEC[:, c, :], in_=psb[5][:, 0:8], func=AF.Exp, scale=-1.0),
                 reads=[psbuf[5]], writes=[gb], partial=True)
            P.op("dve", lambda e, c=c, nt=nt: e.tensor_tensor(out=tmp8[1][0:nt, :], in0=Gall[0:nt, c, 0:8],
                                                              in1=psb[7][0:nt, 0:8], op=ALU.add),
                 reads=[psbuf[7], gb], writes=[gb], partial=True)
            P.op("act", lambda e, c=c, nt=nt: e.activation(out=Wall[0:nt, c, :], in_=tmp8[1][0:nt, :], func=AF.Exp),
                 reads=[gb], writes=[gb], partial=True)
            if c == 0:
                P.op("dve", lambda e, nt=nt: e.tensor_scalar(out=Wall[0:nt, 0, :], in0=Wall[0:nt, 0, :],
                                                            scalar1=flag_t[0:nt, 0:1], scalar2=None, op0=ALU.mult),
                     reads=[gb, cbuf], writes=[gb], partial=True)
            P.op("act", lambda e, c=c, nt=nt: e.activation(out=EB[0:nt, c, :], in_=psb[7][0:nt, 0:8], func=AF.Exp,
                                                           scale=-1.0, bias=-0.5 * math.log(DH)),
                 reads=[psbuf[7]], writes=[gb], partial=True)

        def rot(banks):
            st = {"i": 0}

            def nxt():
                b = banks[st["i"] % len(banks)]
                st["i"] += 1
                return b
            return nxt

        ring_sw = rot([0, 1])

        def gen_sweep(col0, evac):
            wt, wb = W.get(wl[:, col0:col0 + 128], KC, 128)
            for (c0, w) in pcs:
                b = ring_sw()
                mm_group(b, w, [(wt[:, k, :], hTt[:, k, c0:c0 + w]) for k in range(KC)], reads=[hT_buf, wb])
                evac(b, c0, w)
                yield

        def chain(*gens):
            for g in gens:
                for _ in g:
                    yield

        def interleave(a, b, every):
            i = 0
            b_done = b is None
            for _ in a:
                i += 1
                if not b_done and i % every == 0:
                    try:
                        next(b)
                    except StopIteration:
                        b_done = True
            if not b_done:
                for _ in b:
                    pass

        def sweep_chunk(col0, evac, nkc=KC, src=None, rhs_of=None):
            wt, wb = W.get((wl if src is None else src)[:, col0:col0 + 128], nkc, 128)
            for (c0, w) in pcs:
                b = next_bank()
                mm_group(b, w, [(wt[:, k, :], hTt[:, k, c0:c0 + w]) for k in range(nkc)], reads=[hT_buf, wb])
                evac(b, c0, w)

        def ev_copy(dst, dbuf, eng):
            def f(b, c0, w):
                if eng == "act":
                    P.op("act", lambda e: e.activation(out=dst[:, c0:c0 + w], in_=psb[b][:, 0:w], func=AF.Copy),
                         reads=[psbuf[b]], writes=[dbuf], partial=(c0 > 0))
                else:
                    P.op("dve", lambda e: e.tensor_copy(out=dst[:, c0:c0 + w], in_=psb[b][:, 0:w]),
                         reads=[psbuf[b]], writes=[dbuf], partial=(c0 > 0))
            return f

        def ev_sig(dst, dbuf):
            def f(b, c0, w):
                P.op("act", lambda e: e.activation(out=dst[:, c0:c0 + w], in_=psb[b][:, 0:w], func=AF.Sigmoid),
                     reads=[psbuf[b]], writes=[dbuf], partial=(c0 > 0))
            return f

        KV = [[AR.bf16([128, TC]) for _ in range(2)] for _ in range(2)]
        KVb = [[NB() for _ in range(2)] for _ in range(2)]
        Kt1 = [AR.bf16([NCH, 128]) for _ in range(2)]; Kt1b = [[NB() for _ in range(NCH)] for _ in range(2)]
        Vp1 = [AR.bf16([NCH, 130]) for _ in range(2)]; Vp1b = [[NB() for _ in range(NCH)] for _ in range(2)]
        Vp1o = [NB(), NB()]
        Z1 = [AR.f32([128, 130]) for _ in range(2)]
        for i_ in range(2):
            P.op("dve", lambda e, i_=i_: e.memset(Vp1[i_][:, :, :], 0.0), writes=[Vp1o[i_]] + Vp1b[i_])

        r_tk, r_tv, r_u1 = rot([2, 3]), rot([4, 5]), rot([6, 7])

        def sweeps1(j):
            qs = j % 2
            k_, v_ = KV[qs]
            kb_, vb_ = KVb[qs]
            for _ in gen_sweep(1 * MIXW + j * 128, ev_copy(k_, kb_, "act")):
                yield
            for _ in gen_sweep(2 * MIXW + j * 128, ev_copy(v_, vb_, "act")):
                yield
            P.dma("sp", ksc[j], k_[:, :], key=f"ks{qs}", reads=[kb_], writes=[ksc_buf], partial=True)

        def chunks1(j):
            qs = j % 2
            k_, v_ = KV[qs]
            kb_, vb_ = KVb[qs]
            Ktok, Ktc, Vp, Vpc, Vpo = Kt1[qs], Kt1b[qs], Vp1[qs], Vp1b[qs], Vp1o[qs]
            P.op("dve", lambda e: e.tensor_copy(out=Vp[0:64, :, 128], in_=Wall[0:64, :, j]), reads=[gb], writes=[Vpo] + Vpc)
            st = {"zprev": None, "dprev": None, "zpb": None, "ci": 0}
            zbufs = [NB(), NB()]

            def emit_u(c, c0, nt):
                bu = r_u1()
                ci = st["ci"]
                st["ci"] += 1
                P.op("pe", lambda e: e.matmul(psb[bu][:, 0:129], lhsT=Ktok[0:nt, c, :], rhs=Vp[0:nt, c, 0:129], start=True,
                                              stop=True), reads=[Ktc[c], Vpc[c], Vpo], writes=[psbuf[bu]])
                zn = Z1[ci % 2][:, 0:129]
                zprev, dprev, zpb = st["zprev"], st["dprev"], st["zpb"]
                if zprev is None:
                    P.op("dve", lambda e: e.tensor_copy(out=zn, in_=psb[bu][:, 0:129]), reads=[psbuf[bu]],
                         writes=[zbufs[ci % 2]])
                else:
                    P.op("dve", lambda e: e.scalar_tensor_tensor(out=zn, in0=zprev, scalar=dprev, in1=psb[bu][:, 0:129],
                                                                 op0=ALU.mult, op1=ALU.add),
                         reads=[psbuf[bu], zpb, gb], writes=[zbufs[ci % 2]])
                st["zprev"], st["dprev"], st["zpb"] = zn, DEC[:, c, j:j + 1], zbufs[ci % 2]

            pend = None
            for (c, c0, nt) in act_chunks:
                bk, bv = r_tk(), r_tv()
                P.op("pe", lambda e, c0=c0, nt=nt, bk=bk: e.matmul(psb[bk][0:nt, 0:128], lhsT=k_[:, c0:c0 + nt], rhs=idb[:],
                                                                   start=True, stop=True), reads=[kb_, cbuf],
                     writes=[psbuf[bk]])
                P.op("dve", lambda e, c=c, nt=nt, bk=bk: e.tensor_copy(out=Ktok[0:nt, c, :], in_=psb[bk][0:nt, 0:128]),
                     reads=[psbuf[bk]], writes=[Ktc[c]])
                P.op("pe", lambda e, c0=c0, nt=nt, bv=bv: e.matmul(psb[bv][0:nt, 0:128], lhsT=v_[:, c0:c0 + nt], rhs=idb[:],
                                                                   start=True, stop=True), reads=[vb_, cbuf],
                     writes=[psbuf[bv]])
                P.op("dve", lambda e, c=c, nt=nt, bv=bv: e.tensor_scalar(out=Vp[0:nt, c, 0:128], in0=psb[bv][0:nt, 0:128],
                                                                         scalar1=Wall[0:nt, c, j:j + 1], scalar2=None,
                                                                         op0=ALU.mult),
                     reads=[psbuf[bv], gb, Vpo], writes=[Vpc[c]], partial=True)
                if pend is not None:
                    emit_u(*pend)
                pend = (c, c0, nt)
                yield
            emit_u(*pend)
            P.dma("sp", ktsc[j], Ktok[0:64, :, :], key=f"kt{qs}", reads=Ktc, writes=[ksc_buf], partial=True)
            P.dma("sp", vpsc[j], Vp[0:64, :, :], key=f"vp{qs}", reads=Vpc + [Vpo], writes=[ksc_buf], partial=True)
            zprev, dprev, zpb = st["zprev"], st["dprev"], st["zpb"]
            P.op("dve", lambda e: e.tensor_scalar(out=Cfin[:, j, 0:129], in0=zprev, scalar1=dprev, scalar2=None,
                                                  op0=ALU.mult),
                 reads=[zpb, gb], writes=[cfin_buf], partial=(j > 0))

        for _ in sweeps1(0):
            pass
        for j in range(NH):
            interleave(chunks1(j), sweeps1(j + 1) if j + 1 < NH else None, 3)

        P.dma("sp", send_c.rearrange("(h p) c -> p h c", p=128), Cfin[:, :, :], key="xc0", reads=[cfin_buf],
              writes=[sendc_buf])
        P.coll(send_c, recv_c, GROUPS, reads=[sendc_buf], writes=[recvc_buf])
        P.dma("sp", Zcar[:, :, :], recv_c[0:NH * 128, :].rearrange("(h p) c -> p h c", p=128), key="xc1",
              reads=[recvc_buf], writes=[car_buf])
        P.op("dve", lambda e: e.tensor_scalar(out=Zcar[:, :, :], in0=Zcar[:, :, :], scalar1=flag_t[:, 1:2], scalar2=None,
                                              op0=ALU.mult), reads=[car_buf, cbuf], writes=[car_buf], partial=True)
        P.op("dve", lambda e: e.tensor_copy(out=Ccar[:, :, :], in_=Zcar[:, :, :]), reads=[car_buf], writes=[car_buf],
             partial=True)
        P.op("dve", lambda e: e.memset(dcar[:], 1.0), writes=[car_buf], partial=True)

        guard_now(keep=Ystb + [gb])
        AR.off = mark2
        QO = [[AR.bf16([128, TC]) for _ in range(2)] for _ in range(2)]
        QOb = [[NB() for _ in range(2)] for _ in range(2)]
        K2 = [AR.bf16([128, TC]) for _ in range(2)]; K2b = [NB(), NB()]
        Ktok = AR.bf16([NCH, 128]); Ktb = NB()
        Vp = AR.bf16([NCH, 130]); Vpb = NB()
        Cbf = AR.bf16([NCH, 130]); Cbb = [NB() for _ in range(NCH)]
        Hs = AR.f32([NCH, 130]); Hsb = NB()
        Hn = AR.bf16([NCH, 128]); Hnc = [NB() for _ in range(NCH)]
        Sm = AR.bf16([NCH, 64]); Smc = [NB() for _ in range(NCH)]
        SS = AR.f32([128, NCH]); Zb = [AR.f32([128, 130]) for _ in range(2)]
        t64 = [AR.f32([128, NCH]) for _ in range(4)]
        junk = AR.f32([128, 128])

        r_u2, r_s2, r_h2 = rot([2, 3]), rot([4, 5]), rot([6, 7])

        def sweeps2(j):
            qs = j % 2
            q_, o_ = QO[qs]
            qb_, ob_ = QOb[qs]
            k_, kb_ = K2[qs], K2b[qs]
            P.dma("sp", k_[:, :], ksc[j], key=f"k2{qs}", reads=[ksc_buf], writes=[kb_])
            for _ in gen_sweep(0 * MIXW + j * 128, ev_copy(q_, qb_, "act")):
                yield
            for _ in gen_sweep(3 * MIXW + j * 128, ev_sig(o_, ob_)):
                yield

        def chunks2(j):
            qs = j % 2
            q_, o_ = QO[qs]
            qb_, ob_ = QOb[qs]
            k_, kb_ = K2[qs], K2b[qs]
            P.dma("sp", Ktok[0:64, :, :], ktsc[j], key="kt2", reads=[ksc_buf], writes=[Ktb])
            P.dma("sp", Vp[0:64, :, :], vpsc[j], key="vp2", reads=[ksc_buf], writes=[Vpb])
            P.op("dve", lambda e: e.memset(SS[:, :], 0.0), reads=[Hsb], writes=[Hsb])
            zprev = Zcar[:, j, 0:129]
            dprev = dcar[:, j:j + 1]
            zpb = NB()
            zpb.w.update(car_buf.w)
            zbufs = [NB(), NB()]
            cprev = Ccar[:, j, 0:129]
            cpb = car_buf
            pend = None

            def emit_h(c, c0, nt, cprev, cpb):
                bh = r_h2()

                def fn(e):
                    e.matmul(psb[bh][0:nt, 0:129], lhsT=q_[:, c0:c0 + nt], rhs=cprev, start=True, stop=False)
                    return e.matmul(psb[bh][0:nt, 0:129], lhsT=Sm[0:nt, c, 0:nt], rhs=Vp[0:nt, c, 0:129], start=False,
                                    stop=True)

                P.op("pe", fn, reads=[qb_, cpb, Smc[c], Vpb], writes=[psbuf[bh]])
                P.op("act", lambda e: e.activation(out=Hs[0:nt, c, 0:129], in_=psb[bh][0:nt, 0:129], func=AF.Copy),
                     reads=[psbuf[bh]], writes=[Hsb], partial=True)
                P.op("act", lambda e: e.activation(out=junk[0:nt, :], in_=psb[bh][0:nt, 0:128], func=AF.Square,
                                                   accum_out=SS[0:nt, c:c + 1]),
                     reads=[psbuf[bh]], writes=[Hsb], partial=True)

            for ci, (c, c0, nt) in enumerate(act_chunks):
                bu, bs = r_u2(), r_s2()
                P.op("pe", lambda e, c=c, nt=nt, bu=bu: e.matmul(psb[bu][:, 0:129], lhsT=Ktok[0:nt, c, :],
                                                                 rhs=Vp[0:nt, c, 0:129], start=True, stop=True),
                     reads=[Ktb, Vpb], writes=[psbuf[bu]])
                zn = Zb[ci % 2][:, 0:129]
                P.op("dve", lambda e, zn=zn, zprev=zprev, dprev=dprev, bu=bu: e.scalar_tensor_tensor(
                    out=zn, in0=zprev, scalar=dprev, in1=psb[bu][:, 0:129], op0=ALU.mult, op1=ALU.add),
                    reads=[psbuf[bu], zpb, gb], writes=[zbufs[ci % 2]])
                P.op("act", lambda e, zn=zn, c=c: e.activation(out=Cbf[:, c, 0:129], in_=zn, func=AF.Copy,
                                                               scale=DEC[:, c, j:j + 1]),
                     reads=[zbufs[ci % 2], gb], writes=[Cbb[c]])
                zprev, dprev, zpb = zn, DEC[:, c, j:j + 1], zbufs[ci % 2]
                P.op("pe", lambda e, c0=c0, nt=nt, bs=bs: e.matmul(psb[bs][0:nt, 0:nt], lhsT=k_[:, c0:c0 + nt],
                                                                   rhs=q_[:, c0:c0 + nt], start=True, stop=True),
                     reads=[kb_, qb_], writes=[psbuf[bs]])
                P.op("dve", lambda e, c=c, nt=nt, bs=bs: e.tensor_tensor(out=Sm[0:nt, c, 0:nt], in0=psb[bs][0:nt, 0:nt],
                                                                         in1=maskf[0:nt, 0:nt], op=ALU.mult),
                     reads=[psbuf[bs], cbuf], writes=[Smc[c]])
                if pend is not None:
                    emit_h(*pend)
                pend = (c, c0, nt, cprev, cpb)
                cprev, cpb = Cbf[:, c, 0:129], Cbb[c]
                yield
            emit_h(*pend)
            yield
            R = slice(0, NCH)
            ebj = EB[0:64, R, j]
            T0, T1, T2, T3 = [t[0:64, R] for t in t64]
            ssr = SS[0:64, R]
            P.op("dve", lambda e: e.tensor_tensor(out=T0, in0=Hs[0:64, R, 128], in1=ebj, op=ALU.mult),
                 reads=[Hsb, gb], writes=[Hsb], partial=True)
            P.op("act", lambda e: e.activation(out=T0, in_=T0, func=AF.Abs), reads=[Hsb], writes=[Hsb], partial=True)
            P.op("dve", lambda e: e.tensor_single_scalar(out=T0, in_=T0, scalar=1.0, op=ALU.max),
                 reads=[Hsb], writes=[Hsb], partial=True)
            P.op("dve", lambda e: e.reciprocal(out=T0, in_=T0), reads=[Hsb], writes=[Hsb], partial=True)
            P.op("dve", lambda e: e.tensor_tensor(out=T1, in0=ebj, in1=T0, op=ALU.mult),
                 reads=[Hsb], writes=[Hsb], partial=True)
            P.op("dve", lambda e: e.tensor_tensor(out=T2, in0=T1, in1=T1, op=ALU.mult),
                 reads=[Hsb], writes=[Hsb], partial=True)
            P.op("dve", lambda e: e.tensor_tensor(out=T2, in0=T2, in1=ssr, op=ALU.mult),
                 reads=[Hsb], writes=[Hsb], partial=True)
            P.op("act", lambda e: e.activation(out=T2, in_=T2, func=AF.Ln, scale=1.0 / DH, bias=EPS),
                 reads=[Hsb], writes=[Hsb], partial=True)
            P.op("act", lambda e: e.activation(out=T2, in_=T2, func=AF.Exp, scale=-0.5),
                 reads=[Hsb], writes=[Hsb], partial=True)
            P.op("dve", lambda e: e.tensor_tensor(out=T3, in0=T1, in1=T2, op=ALU.mult),
                 reads=[Hsb], writes=[Hsb], partial=True)
            ys = j % 2

            def emit_hn(c, c0, nt):
                P.op("dve", lambda e: e.tensor_scalar(out=Hn[0:nt, c, :], in0=Hs[0:nt, c, 0:128],
                                                      scalar1=t64[3][0:nt, c:c + 1], scalar2=None, op0=ALU.mult),
                     reads=[Hsb], writes=[Hnc[c]])

            emit_hn(*act_chunks[0])
            for ci, (c, c0, nt) in enumerate(act_chunks):
                bt = r_u2()
                if ci + 1 < len(act_chunks):
                    emit_hn(*act_chunks[ci + 1])
                P.op("pe", lambda e, c=c, nt=nt, bt=bt: e.matmul(psb[bt][:, 0:nt], lhsT=Hn[0:nt, c, :], rhs=idb[0:nt, 0:nt],
                                                                 start=True, stop=True), reads=[Hnc[c], cbuf],
                     writes=[psbuf[bt]])
                P.op("dve", lambda e, c0=c0, nt=nt, bt=bt: e.scalar_tensor_tensor(
                    out=Yst[ys][:, c0:c0 + nt], in0=psb[bt][:, 0:nt], scalar=vcol(l, cfg.v_gain, j),
                    in1=o_[:, c0:c0 + nt], op0=ALU.mult, op1=ALU.mult),
                    reads=[psbuf[bt], ob_, cbuf], writes=[Ystb[ys]], partial=True)
                yield
            P.dma("sp", ysc[j * 128:(j + 1) * 128, g0 + lo:g0 + TC], Yst[ys][:, lo:TC], key=f"ys{ys}",
                  reads=[Ystb[ys]], writes=[ysc_buf], partial=True)

        for _ in sweeps2(0):
            pass
        for j in range(NH):
            interleave(chunks2(j), sweeps2(j + 1) if j + 1 < NH else None, 6)

        guard_now(keep=Ystb)
        AR.off = mark
        CB = AR.f32([128, TC]); CC = AR.f32([128, TC]); PR = AR.f32([128, TC + 2]); T1c = AR.f32([128, TC])
        cbb, ccb, prb, t1b = NB(), NB(), NB(), NB()
        P.op("dve", lambda e: e.memset(PR[:, 0:2], 0.0), writes=[prb])
        for c in range(8):
            def ev_cc(b, c0, w):
                P.op("act", lambda e: e.activation(out=CC[:, c0:c0 + w], in_=psb[b][:, 0:w], func=AF.Copy),
                     reads=[psbuf[b]], writes=[ccb], partial=(c0 > 0))

            def ev_cx(b, c0, w):
                P.op("dve", lambda e: e.tensor_tensor(out=PR[:, 2 + c0:2 + c0 + w], in0=psb[b][:, 0:w],
                                                      in1=CC[:, c0:c0 + w], op=ALU.mult),
                     reads=[psbuf[b], ccb], writes=[prb], partial=True)

            def ev_cb(b, c0, w):
                P.op("act", lambda e: e.activation(out=CB[:, c0:c0 + w], in_=psb[b][:, 0:w], func=AF.Copy),
                     reads=[psbuf[b]], writes=[cbb], partial=(c0 > 0))

            sweep_chunk(4 * MIXW + 16 + 1 * MIXW + c * 128, ev_cc)
            sweep_chunk(4 * MIXW + 16 + 2 * MIXW + c * 128, ev_cx)
            sweep_chunk(4 * MIXW + 16 + 0 * MIXW + c * 128, ev_cb)
            w0, w1, w2 = [vcol(l, cfg.v_convw, jj * 8 + c) for jj in range(3)]
            P.op("dve", lambda e, w0=w0: e.tensor_scalar(out=T1c[:, :], in0=PR[:, 0:TC], scalar1=w0, scalar2=None,
                                                         op0=ALU.mult), reads=[prb, cbuf], writes=[t1b])
            P.op("dve", lambda e, w1=w1: e.scalar_tensor_tensor(out=T1c[:, :], in0=PR[:, 1:TC + 1], scalar=w1,
                                                                in1=T1c[:, :], op0=ALU.mult, op1=ALU.add),
                 reads=[prb, cbuf], writes=[t1b], partial=True)
            P.op("dve", lambda e, w2=w2: e.scalar_tensor_tensor(out=T1c[:, :], in0=PR[:, 2:TC + 2], scalar=w2,
                                                                in1=T1c[:, :], op0=ALU.mult, op1=ALU.add),
                 reads=[prb, cbuf], writes=[t1b], partial=True)
            ys = c % 2
            P.op("dve", lambda e, ys=ys: e.tensor_tensor(out=Yst[ys][:, :], in0=T1c[:, :], in1=CB[:, :], op=ALU.mult),
                 reads=[t1b, cbb], writes=[Ystb[ys]])
            P.dma("sp", ysc[MIXW + c * 128:MIXW + (c + 1) * 128, g0 + lo:g0 + TC], Yst[ys][:, lo:TC], key=f"ys{ys}",
                  reads=[Ystb[ys]], writes=[ysc_buf], partial=True)

        guard_now(keep=Ystb)
        AR.off = mark
        PAD = 16
        L = PAD + TC
        X = [AR.f32([128, L]) for _ in range(5)]
        Xb = [NB() for _ in range(5)]
        Pp = [AR.bf16([128, TC]) for _ in range(2)]
        Ppb = [NB(), NB()]
        P.op("dve", lambda e: e.memset(X[0][:, 0:PAD], 0.0), writes=[Xb[0]])
        for c in range(8):
            g = c // 2
            def ev_pu(b, c0, w):
                P.op("act", lambda e: e.activation(out=X[0][:, PAD + c0:PAD + c0 + w], in_=psb[b][:, 0:w], func=AF.Copy),
                     reads=[psbuf[b]], writes=[Xb[0]], partial=True)
            sweep_chunk(4 * MIXW + 16 + 3 * MIXW + c * 128, ev_pu)
            for lv in range(1, g + 2):
                sh = 1 << (lv - 1)
                st = (1 << lv) - 1
                P.op("dve", lambda e, lv=lv, sh=sh, st=st: e.tensor_tensor(out=X[lv][:, st:L], in0=X[lv - 1][:, st:L],
                                                                          in1=X[lv - 1][:, st - sh:L - sh], op=ALU.add),
                     reads=[Xb[lv - 1]], writes=[Xb[lv]])
            top = g + 1
            wdw = POOL_W[g]
            pp = Pp[c % 2]
            P.op("dve", lambda e, top=top, wdw=wdw, pp=pp: e.scalar_tensor_tensor(
                out=pp[:, N_META:TC], in0=X[top][:, PAD + N_META:L], scalar=1.0 / wdw, in1=X[0][:, PAD + N_META:L],
                op0=ALU.mult, op1=ALU.subtract), reads=[Xb[top], Xb[0]], writes=[Ppb[c % 2]])
            if s == 0:
                P.op("dve", lambda e, top=top, g=g: e.tensor_tensor(out=X[top][:, PAD:PAD + N_META],
                                                                    in0=X[top][:, PAD:PAD + N_META],
                                                                    in1=invc_t[:, g * 16:(g + 1) * 16], op=ALU.mult),
                     reads=[Xb[top], cbuf, Ppb[c % 2]], writes=[Xb[top]], partial=True)
                P.op("dve", lambda e, top=top, pp=pp: e.tensor_tensor(out=pp[:, 0:N_META], in0=X[top][:, PAD:PAD + N_META],
                                                                      in1=X[0][:, PAD:PAD + N_META], op=ALU.subtract),
                     reads=[Xb[top], Xb[0]], writes=[Ppb[c % 2]], partial=True)
            if c % 2 == 1:
                pw, pwb = W.get(pool_w[l][g * 256:(g + 1) * 256, :], 2, 256)
                for dch in range(2):
                    ys = dch
                    for (c0, w) in pcs:
                        b = next_bank()
                        mm_group(b, w, [(pw[:, k, dch * 128:(dch + 1) * 128], Pp[k][:, c0:c0 + w]) for k in range(2)],
                                 reads=[Ppb[0], Ppb[1], pwb])
                        P.op("act", lambda e, b=b, c0=c0, w=w, ys=ys, g=g, dch=dch: e.activation(
                            out=Yst[ys][:, c0:c0 + w], in_=psb[b][:, 0:w], func=AF.Copy,
                            scale=vcol(l, cfg.v_pscale, g * 2 + dch)),
                            reads=[psbuf[b], cbuf], writes=[Ystb[ys]], partial=(c0 > 0))
                    r0 = 2 * MIXW + g * 256 + dch * 128
                    P.dma("sp", ysc[r0:r0 + 128, g0 + lo:g0 + TC], Yst[ys][:, lo:TC], key=f"ys{ys}",
                          reads=[Ystb[ys]], writes=[ysc_buf], partial=True)

        for m in range(3 * KC):
            ys = m % 2

            def ev_g(b, c0, w, ys=ys):
                P.op("act", lambda e: e.activation(out=Yst[ys][:, c0:c0 + w], in_=psb[b][:, 0:w], func=AF.Sigmoid),
                     reads=[psbuf[b]], writes=[Ystb[ys]], partial=(c0 > 0))

            sweep_chunk(8 * MIXW + 16 + m * 128, ev_g)
            P.dma("sp", gsc[m * 128:(m + 1) * 128, g0 + lo:g0 + TC], Yst[ys][:, lo:TC], key=f"ys{ys}",
                  reads=[Ystb[ys]], writes=[gsc_buf], partial=True)

    car_buf = Buf()

    def mixer_p2(l, g0, lo, n):
        phase_begin()
        pcs = even_split(n, 512)
        Y = AR.bf16([24, n]); Yb = NB()
        Gt = [AR.bf16([3, n]) for _ in range(2)]; Gtb = [NB(), NB()]
        Mt = [AR.f32([128, 512]) for _ in range(3)]; Mtb = [NB(), NB(), NB()]
        Ost = [AR.f32([128, n]) for _ in range(2)]; Ostb = [NB(), NB()]
        a, b_ = g0 + lo, g0 + lo + n
        P.dma("sp", Y[:], ysc.rearrange("(k p) t -> p k t", p=128)[:, :, a:b_], key="yl", reads=[ysc_buf],
              writes=[Yb])
        mg = hTt
        for dm in range(KC):
            gs = dm % 2
            P.dma("sp", Gt[gs][:], gsc.rearrange("(r k p) t -> p r k t", p=128, r=3)[:, :, dm, a:b_], key=f"gl{gs}",
                  reads=[gsc_buf], writes=[Gtb[gs]])
            wt01 = W.get(w_branch[l][0:2 * MIXW, dm * 128:(dm + 1) * 128], 16, 128)
            wt2 = W.get(w_branch[l][2 * MIXW:3 * MIXW, dm * 128:(dm + 1) * 128], 8, 128)
            wts = [(wt01[0], wt01[1], 0), (wt01[0], wt01[1], 8), (wt2[0], wt2[1], 0)]
            for (c0, w) in pcs:
                for r in range(3):
                    wt, wb, ko = wts[r]
                    b = next_bank()
                    mm_group(b, w, [(wt[:, ko + k, :], Y[:, r * 8 + k, c0:c0 + w]) for k in range(8)], reads=[Yb, wb])
                    P.op("dve", lambda e, b=b, r=r, c0=c0, w=w, gs=gs: e.tensor_tensor(
                        out=Mt[r][:, 0:w], in0=psb[b][:, 0:w], in1=Gt[gs][:, r, c0:c0 + w], op=ALU.mult),
                        reads=[psbuf[b], Gtb[gs]], writes=[Mtb[r]])
                P.op("dve", lambda e, w=w: e.tensor_tensor(out=Mt[0][:, 0:w], in0=Mt[0][:, 0:w], in1=Mt[1][:, 0:w],
                                                           op=ALU.add), reads=[Mtb[1]], writes=[Mtb[0]], partial=True)
                P.op("dve", lambda e, w=w, c0=c0, dm=dm: e.tensor_tensor(out=mg[:, dm, lo + c0:lo + c0 + w],
                                                                         in0=Mt[0][:, 0:w], in1=Mt[2][:, 0:w], op=ALU.add),
                     reads=[Mtb[0], Mtb[2]], writes=[hT_buf], partial=True)
        for dm in range(KC):
            wt, wb = W.get(w_out[l][:, dm * 128:(dm + 1) * 128], KC, 128)
            os_ = dm % 2
            for (c0, w) in pcs:
                b = next_bank()
                mm_group(b, w, [(wt[:, k, :], mg[:, k, lo + c0:lo + c0 + w]) for k in range(KC)], reads=[hT_buf, wb])
                P.op("act", lambda e, b=b, c0=c0, w=w, os_=os_: e.activation(out=Ost[os_][:, c0:c0 + w],
                                                                           in_=psb[b][:, 0:w], func=AF.Copy),
                     reads=[psbuf[b]], writes=[Ostb[os_]], partial=(c0 > 0))
            P.dma("sp", oT[dm * 128:(dm + 1) * 128, a:b_], Ost[os_][:, :], key=f"os{os_}", reads=[Ostb[os_]],
                  writes=[oT_buf], partial=True)

    def ffn(l, g0, lo, n, first):
        phase_begin()
        pcs = even_split(n, 512)
        gT = AR.bf16([FC, n]); gTb = NB()
        A_ = AR.f32([128, n + 2]); A = [A_, A_]; Ab_ = NB(); Ab = [Ab_, Ab_]
        Tt = [AR.f32([128, n]) for _ in range(2)]; Ttb = [NB(), NB()]
        Ost = Tt; Ostb = Ttb
        a_, b_ = g0 + lo, g0 + lo + n
        wl = w_ffn_in[l]
        if first:
            P.op("dve", lambda e: e.memset(acar[:], 0.0), reads=[acar_buf], writes=[acar_buf])
        for c in range(FC):
            sl = c % 2
            wa, wab = W.get(wl[:, c * 128:(c + 1) * 128], KC, 128)
            wu, wub = W.get(wl[:, DFF + c * 128:DFF + (c + 1) * 128], KC, 128)
            P.op("dve", lambda e, c=c, sl=sl: e.tensor_copy(out=A[sl][:, 0:2], in_=acar[:, c, :]),
                 reads=[acar_buf, gTb], writes=[Ab[sl]])
            for (c0, w) in pcs:
                b = next_bank()
                mm_group(b, w, [(wa[:, k, :], hTt[:, k, lo + c0:lo + c0 + w]) for k in range(KC)], reads=[hT_buf, wab])
                P.op("act", lambda e, b=b, c0=c0, w=w, sl=sl: e.activation(out=A[sl][:, 2 + c0:2 + c0 + w],
                                                                         in_=psb[b][:, 0:w], func=AF.Copy),
                     reads=[psbuf[b]], writes=[Ab[sl]], partial=True)
            P.op("dve", lambda e, c=c, sl=sl: e.tensor_copy(out=acar[:, c, :], in_=A[sl][:, n:n + 2]),
                 reads=[Ab[sl]], writes=[acar_buf], partial=True)
            w0, w1, w2 = [vcol(l, cfg.v_fconv, jj * FC + c) for jj in range(3)]
            P.op("dve", lambda e, sl=sl, w0=w0: e.tensor_scalar(out=Tt[sl][:, :], in0=A[sl][:, 0:n], scalar1=w0,
                                                                scalar2=None, op0=ALU.mult),
                 reads=[Ab[sl], cbuf], writes=[Ttb[sl]])
            P.op("dve", lambda e, sl=sl, w1=w1: e.scalar_tensor_tensor(out=Tt[sl][:, :], in0=A[sl][:, 1:n + 1], scalar=w1,
                                                                       in1=Tt[sl][:, :], op0=ALU.mult, op1=ALU.add),
                 reads=[Ab[sl], cbuf], writes=[Ttb[sl]], partial=True)
            P.op("dve", lambda e, sl=sl, w2=w2: e.scalar_tensor_tensor(out=Tt[sl][:, :], in0=A[sl][:, 2:n + 2], scalar=w2,
                                                                       in1=Tt[sl][:, :], op0=ALU.mult, op1=ALU.add),
                 reads=[Ab[sl], cbuf], writes=[Ttb[sl]], partial=True)
            P.op("act", lambda e, sl=sl: e.activation(out=Tt[sl][:, :], in_=Tt[sl][:, :], func=AF.Gelu_apprx_tanh),
                 reads=[Ttb[sl]], writes=[Ttb[sl]], partial=True)
            for (c0, w) in pcs:
                b = next_bank()
                mm_group(b, w, [(wu[:, k, :], hTt[:, k, lo + c0:lo + c0 + w]) for k in range(KC)], reads=[hT_buf, wub])
                P.op("dve", lambda e, b=b, c0=c0, w=w, sl=sl, c=c: e.tensor_tensor(
                    out=gT[:, c, c0:c0 + w], in0=psb[b][:, 0:w], in1=Tt[sl][:, c0:c0 + w], op=ALU.mult),
                    reads=[psbuf[b], Ttb[sl]], writes=[gTb], partial=True)
        wlo = w_ffn_out[l]
        kparts = [(k0, min(16, FC - k0)) for k0 in range(0, FC, 16)]
        for dm in range(KC):
            wts = [W.get(wlo[k0 * 128:(k0 + kn) * 128, dm * 128:(dm + 1) * 128], kn, 128) for (k0, kn) in kparts]
            os_ = dm % 2
            for (c0, w) in pcs:
                b = next_bank()
                pairs = []
                for (k0, kn), (wt, wb) in zip(kparts, wts):
                    pairs += [(wt[:, k, :], gT[:, k0 + k, c0:c0 + w]) for k in range(kn)]
                mm_group(b, w, pairs, reads=[gTb] + [wb for (_, wb) in wts])
                P.op("act", lambda e, b=b, c0=c0, w=w, os_=os_: e.activation(out=Ost[os_][:, c0:c0 + w],
                                                                           in_=psb[b][:, 0:w], func=AF.Copy),
                     reads=[psbuf[b]], writes=[Ostb[os_]], partial=(c0 > 0))
            P.dma("sp", oT[dm * 128:(dm + 1) * 128, a_:b_], Ost[os_][:, :], key=f"os{os_}", reads=[Ostb[os_]],
                  writes=[oT_buf], partial=True)

    acar_buf = Buf()


    def xchg_x():
        phase_begin()
        XS = AR.f32([KC, 16]); XR = AR.f32([KC, 16]); XP = AR.f32([KC, 16])
        xsb, xrb, xpb = NB(), NB(), NB()
        xv = xT.rearrange("(k p) t -> p k t", p=128)
        P.dma("sp", XS[:], xv[:, :, TS - 16:TS], key="xx0", reads=[xT_buf], writes=[xsb])
        P.dma("sp", send_x.rearrange("(k p) c -> p k c", p=128), XS[:], key="xx1", reads=[xsb], writes=[sendx_buf])
        P.coll(send_x, recv_x, GROUPS, reads=[sendx_buf], writes=[recvx_buf])
        P.dma("sp", XR[:], recv_x[0:D, :].rearrange("(k p) c -> p k c", p=128), key="xx2", reads=[recvx_buf],
              writes=[xrb])
        P.dma("sp", XP[:], xv[:, :, 0:16], key="xx3", reads=[xT_buf], writes=[xpb])
        P.op("dve", lambda e: e.tensor_scalar(out=XP[:], in0=XP[:], scalar1=flag_t[:, 0:1], scalar2=None, op0=ALU.mult),
             reads=[xpb, cbuf], writes=[xpb], partial=True)
        P.op("dve", lambda e: e.scalar_tensor_tensor(out=XP[:], in0=XR[:], scalar=flag_t[:, 1:2], in1=XP[:],
                                                     op0=ALU.mult, op1=ALU.add),
             reads=[xrb, xpb, cbuf], writes=[xpb], partial=True)
        P.dma("sp", xv[:, :, 0:16], XP[:], key="xx4", reads=[xpb], writes=[xT_buf], partial=True)

    def whole():
        transpose_in()
        tiles = [(0, 0, TC // 2), (0, TC // 2, TC - TC // 2)]
        if cfg.stop > 0:
            norm_pass(0, cfg.v_pre_mix, 0, 0, TC)
        for l in range(DEPTH):
            if cfg.stop <= 2 * l:
                break
            if l > 0:
                xchg_x()
                norm_pass(l, cfg.v_pre_mix, 0, 0, N_META)
            mixer_p1(l, 0)
            for (g0, slo, sn) in tiles:
                mixer_p2(l, g0, slo, sn)
                update_pass(l, cfg.v_post_mix, g0, slo, sn, nxt=(l, cfg.v_pre_ffn))
            if cfg.stop <= 2 * l + 1:
                break
            xchg_x()
            norm_pass(l, cfg.v_pre_ffn, 0, 0, N_META)
            for ti, (g0, slo, sn) in enumerate(tiles):
                ffn(l, g0, slo, sn, first=(ti == 0))
                update_pass(l, cfg.v_post_ffn, g0, slo, sn,
                            nxt=(l + 1, cfg.v_pre_mix) if (l + 1 < DEPTH and cfg.stop > 2 * l + 2) else None)
        transpose_out()

    PERS = psbuf + [hT_buf, xT_buf, oT_buf, ysc_buf, gsc_buf, acar_buf, car_buf, cbuf, gd_buf, ksc_buf, cfin_buf, sendc_buf, recvc_buf, sendx_buf, recvx_buf] + W.slots
    snap = [(dict(b.w.d), dict(b.r.d)) for b in PERS]
    saved = (P.streams, P.dma_cnt, ring["i"], guard["tok"])
    P.dry = True
    P.streams = {e: [] for e in ENGS}
    P.dma_cnt = {}
    whole()
    P.streams, P.dma_cnt, ring["i"], guard["tok"] = saved
    P.dry = False
    P.ncoll = 0
    W.i = 0
    del LIVE[:]
    for b, (w_, r_) in zip(PERS, snap):
        b.w.d = dict(w_)
        b.r.d = dict(r_)
    whole()
    P.emit()
    return nc, es, P


def host_vecs(cfg, inp):
    KC, FC = cfg.KC, cfg.FC
    v = np.zeros((128, cfg.DEPTH * cfg.NV), np.float32)
    for l in range(cfg.DEPTH):
        o = l * cfg.NV

        def put(off, arr, n):
            v[:, o + off:o + off + n] = np.asarray(arr, np.float32).reshape(n, 128).T

        put(cfg.v_pre_mix, inp["norm_pre_mix"][l], KC)
        put(cfg.v_post_mix, inp["norm_post_mix"][l], KC)
        put(cfg.v_pre_ffn, inp["norm_pre_ffn"][l], KC)
        put(cfg.v_post_ffn, inp["norm_post_ffn"][l], KC)
        put(cfg.v_gain, inp["mlstm_head_gain"][l], 8)
        put(cfg.v_convw, inp["conv_mix_w"][l].reshape(-1), 24)
        put(cfg.v_pscale, inp["pool_scale"][l], 8)
        put(cfg.v_fconv, inp["ffn_conv_w"][l].reshape(-1), 3 * FC)
    bif = np.zeros((64, cfg.DEPTH * 16), np.float32)
    for l in range(cfg.DEPTH):
        bif[:, l * 16:l * 16 + 16] = np.asarray(inp["b_if"][l], np.float32).reshape(1, 16)
    invc = np.zeros((128, 64), np.float32)
    for g, w in enumerate(POOL_W):
        invc[:, g * 16:(g + 1) * 16] = (1.0 / np.minimum(np.arange(16) + 1, w)).astype(np.float32)[None, :]
    return v, bif, invc


_CACHE = {}


def run(cfg, inp):
    key = (cfg.D, cfg.DFF, cfg.NT, cfg.NSUP, cfg.DEPTH, cfg.stop, cfg.n_cores)
    if key not in _CACHE:
        _CACHE[key] = build_program(cfg)
    nc, es, P = _CACHE[key]
    B = inp["x"].shape[0]
    n_cores = cfg.n_cores
    assert n_cores == 2 * B
    NT = cfg.NT
    v, bif, invc = host_vecs(cfg, inp)
    f = lambda a: np.ascontiguousarray(np.asarray(a, np.float32))
    common = {
        "vecs": v, "bif": bif, "invc": invc,
        "w_in": f(inp["w_in"]),
        "pool_w": f(inp["pool_w"]).reshape(cfg.DEPTH, 1024, 256),
        "w_branch": f(inp["w_branch"]).reshape(cfg.DEPTH, 3 * MIXW, cfg.D),
        "w_out": f(inp["w_out"]),
        "w_ffn_in": f(inp["w_ffn_in"]),
        "w_ffn_out": f(inp["w_ffn_out"]),
    }
    meta = f(inp["meta_tokens"])
    x = f(inp["x"])
    in_maps = []
    for c in range(n_cores):
        b, h = c // 2, c % 2
        m = dict(common)
        pre = meta if h == 0 else x[b, NT - N_META:NT]
        m["xin"] = np.ascontiguousarray(np.concatenate([pre, x[b, h * NT:(h + 1) * NT]], axis=0))
        fl = np.zeros((128, 2), np.float32)
        fl[:, h] = 1.0
        m["flags"] = fl
        in_maps.append(m)
    res = run_bass_kernel_spmd(nc, in_maps, core_ids=list(range(n_cores)))
    if getattr(cfg, "debug", False):
        cfg.dbg = res.results
    outs = [np.asarray(res.results[c]["out"], np.float32) for c in range(n_cores)]
    return np.stack([np.concatenate([outs[2 * b], outs[2 * b + 1]], axis=0) for b in range(B)], axis=0)


def kernel(**inputs):
    cfg = Cfg()
    return run(cfg, inputs)
```

```python
import math
from contextlib import ExitStack

import numpy as np
import concourse.bass as bass
import concourse.mybir as mybir
from concourse.bass_utils import run_bass_kernel_spmd

F32 = mybir.dt.float32
BF16 = mybir.dt.bfloat16
AF = mybir.ActivationFunctionType
ALU = mybir.AluOpType

N_META = 16
MIXW = 1024
NH = 8
DH = 128
CH = 64
EPS = 1e-6
POOL_W = (2, 4, 8, 16)
EPOCH = 24000
ENGS = ("pe", "act", "dve", "pool", "sp")


class Tok:
    __slots__ = ("kind", "key", "pos", "sem", "val", "used")

    def __init__(self, kind, key, pos):
        self.kind, self.key, self.pos = kind, key, pos
        self.sem = None
        self.val = None
        self.used = False


class TokSet:
    __slots__ = ("d",)

    def __init__(self):
        self.d = {}

    def add(self, t):
        o = self.d.get(t.key)
        if o is None or o.pos < t.pos:
            self.d[t.key] = t

    def update(self, other):
        for t in other.d.values():
            self.add(t)

    def toks(self):
        return list(self.d.values())


class Buf:
    __slots__ = ("w", "r")

    def __init__(self):
        self.w = TokSet()
        self.r = TokSet()


class Prog:
    def __init__(self, nc, es):
        self.nc, self.es = nc, es
        self.streams = {e: [] for e in ENGS}
        self.dma_cnt = {}
        self.dry = False
        self.nwaits = 0

    def _deps(self, reads, writes, deps):
        s = TokSet()
        for t in deps:
            if t is not None:
                s.add(t)
        for b in reads:
            s.update(b.w)
        for b in writes:
            s.update(b.w)
            s.update(b.r)
        return s.toks()

    def _post(self, tok, reads, writes, partial):
        for b in reads:
            b.r.add(tok)
        for b in writes:
            if partial:
                b.w.add(tok)
            else:
                b.w = TokSet()
                b.w.add(tok)
                b.r = TokSet()

    def op(self, eng, fn, reads=(), writes=(), deps=(), partial=False):
        st = self.streams[eng]
        tok = Tok("e", ("e", eng), len(st))
        d = self._deps(reads, writes, deps)
        for t in d:
            t.used = True
        st.append((fn, d, tok))
        self._post(tok, reads, writes, partial)
        return tok

    def dma(self, q, out, in_, key, reads=(), writes=(), deps=(), partial=False):
        st = self.streams[q]
        n = self.dma_cnt.get(key, 0) + 1
        self.dma_cnt[key] = n
        tok = Tok("d", ("d", key), n)
        tok.val = 16 * n
        d = self._deps(reads, writes, deps)
        for t in d:
            t.used = True

        def fn(e, out=out, in_=in_):
            return e.dma_start(out=out, in_=in_)

        st.append((fn, d, tok))
        self._post(tok, reads, writes, partial)
        return tok

    def coll(self, src, dst, groups, reads=(), writes=()):
        st = self.streams["pool"]
        self.ncoll = getattr(self, "ncoll", 0) + 1
        tok = Tok("c", ("c", self.ncoll), 1)
        tok.val = 1
        d = self._deps(reads, writes, ())
        for t in d:
            t.used = True

        def fn(e, src=src, dst=dst, groups=groups):
            return e.collective_compute("AllGather", ALU.bypass, replica_groups=groups, ins=[src], outs=[dst])

        st.append((fn, d, tok))
        self._post(tok, reads, writes, False)
        return tok

    def emit(self):
        nc, es = self.nc, self.es
        esems = {}
        for e in ENGS:
            cnt = 0
            for (_, _, tok) in self.streams[e]:
                if tok.kind == "e" and tok.used:
                    ep, v = divmod(cnt, EPOCH)
                    cnt += 1
                    k = (e, ep)
                    if k not in esems:
                        esems[k] = es.enter_context(nc.semaphore(f"s_{e}_{ep}"))
                    tok.sem, tok.val = esems[k], v + 1
        dsems = {}
        for key in self.dma_cnt:
            assert 16 * self.dma_cnt[key] < 60000, (key, self.dma_cnt[key])
            dsems[key] = es.enter_context(nc.semaphore(f"d_{len(dsems)}"))
        for e in ENGS:
            for (_, _, tok) in self.streams[e]:
                if tok.kind == "d":
                    tok.sem = dsems[tok.key[1]]
                elif tok.kind == "c":
                    tok.sem = es.enter_context(nc.semaphore(f"c_{tok.key[1]}"))
        block = es.enter_context(nc.Block())
        prog = self

        def run(e, name):
            waited = {}
            for (fn, deps, tok) in prog.streams[name]:
                for t in deps:
                    sid = id(t.sem)
                    if waited.get(sid, 0) >= t.val:
                        continue
                    e.wait_ge(t.sem, t.val)
                    prog.nwaits += 1
                    waited[sid] = t.val
                ins = fn(e)
                if tok.kind == "d":
                    ins.then_inc(tok.sem, 16)
                elif tok.kind == "c":
                    ins.then_inc(tok.sem, 1)
                elif tok.used:
                    ins.then_inc(tok.sem, 1)

        @block.gpsimd
        def _(e):
            run(e, "pool")

        @block.sync
        def _(e):
            run(e, "sp")

        @block.scalar
        def _(e):
            run(e, "act")

        @block.vector
        def _(e):
            run(e, "dve")

        @block.tensor
        def _(e):
            run(e, "pe")


def even_split(n, maxw=512):
    k = -(-n // maxw)
    assert n % 2 == 0
    h = n // 2
    base, rem = divmod(h, k)
    sizes = [2 * (base + (1 if i < rem else 0)) for i in range(k)]
    out, o = [], 0
    for s in sizes:
        out.append((o, s))
        o += s
    return out


class Cfg:
    def __init__(self, D=2048, DFF=5632, NT=2048, NSUP=1, DEPTH=2, stop=99, n_cores=8):
        self.stop = stop
        self.n_cores = n_cores
        assert NSUP == 1
        self.D, self.DFF, self.NT, self.NSUP, self.DEPTH = D, DFF, NT, NSUP, DEPTH
        self.KC = D // 128
        self.FC = DFF // 128
        self.TC = N_META + NT
        self.TS = N_META + NSUP * NT
        self.INC = 8 * MIXW + 2 * NH + 3 * D
        self.NCH = 1 + NT // CH
        o = 0
        self.v_pre_mix = o; o += self.KC
        self.v_post_mix = o; o += self.KC
        self.v_pre_ffn = o; o += self.KC
        self.v_post_ffn = o; o += self.KC
        self.v_gain = o; o += 8
        self.v_convw = o; o += 24
        self.v_pscale = o; o += 8
        self.v_fconv = o; o += 3 * self.FC
        self.NV = o


def build_program(cfg):
    D, DFF, NT, NSUP, DEPTH = cfg.D, cfg.DFF, cfg.NT, cfg.NSUP, cfg.DEPTH
    KC, FC, TC, TS, NCH = cfg.KC, cfg.FC, cfg.TC, cfg.TS, cfg.NCH
    nc = bass.Bass("TRN2", target_bir_lowering=False)
    es = ExitStack()
    P = Prog(nc, es)
    GROUPS = [[2 * i, 2 * i + 1] for i in range(cfg.n_cores // 2)]

    def din(name, shape, dt=F32):
        return nc.dram_tensor(name, list(shape), dt, kind="ExternalInput").ap()

    xin = din("xin", [TS, D])
    vecs = din("vecs", [128, DEPTH * cfg.NV])
    bif = din("bif", [64, DEPTH * 16])
    invc = din("invc", [128, 4 * 16])
    w_in = din("w_in", [DEPTH, D, cfg.INC])
    pool_w = din("pool_w", [DEPTH, 1024, 256])
    w_branch = din("w_branch", [DEPTH, 3 * MIXW, D])
    w_out = din("w_out", [DEPTH, D, D])
    w_ffn_in = din("w_ffn_in", [DEPTH, D, 2 * DFF])
    w_ffn_out = din("w_ffn_out", [DEPTH, DFF, D])
    out = nc.dram_tensor("out", [TS - N_META, D], F32, kind="ExternalOutput").ap()
    ik = "ExternalOutput" if getattr(cfg, "debug", False) else "Internal"
    dbg = nc.dram_tensor("dbg", [128, 8192], F32, kind=ik).ap()
    dbg2 = nc.dram_tensor("dbg2", [128, 8192], BF16, kind=ik).ap()
    xT = nc.dram_tensor("xT", [D, TS], F32, kind=ik).ap()
    oT = nc.dram_tensor("oT", [D, TS], F32, kind=ik).ap()
    ysc = nc.dram_tensor("ysc", [3 * MIXW, TS], BF16, kind=ik).ap()
    gsc = nc.dram_tensor("gsc", [3 * D, TS], BF16, kind=ik).ap()
    ksc = nc.dram_tensor("ksc", [NH, 128, TC], BF16, kind="Internal").ap()
    ktsc = nc.dram_tensor("ktsc", [NH, 64, NCH, 128], BF16, kind="Internal").ap()
    vpsc = nc.dram_tensor("vpsc", [NH, 64, NCH, 130], BF16, kind="Internal").ap()
    send_c = nc.dram_tensor("send_c", [NH * 128, 130], F32, kind="Internal").ap()
    recv_c = nc.dram_tensor("recv_c", [2 * NH * 128, 130], F32, kind="Internal").ap()
    send_x = nc.dram_tensor("send_x", [D, 16], F32, kind="Internal").ap()
    recv_x = nc.dram_tensor("recv_x", [2 * D, 16], F32, kind="Internal").ap()
    flags = din("flags", [128, 2])

    NSLOT = 6
    hTt = es.enter_context(nc.sbuf_tensor("hT", [128, KC, TC], BF16))
    wring = es.enter_context(nc.sbuf_tensor("wring", [128, NSLOT, 2048], BF16))
    vec_t = es.enter_context(nc.sbuf_tensor("vec_t", [128, DEPTH * cfg.NV], F32))
    bif_t = es.enter_context(nc.sbuf_tensor("bif_t", [64, DEPTH * 16], F32))
    invc_t = es.enter_context(nc.sbuf_tensor("invc_t", [128, 64], F32))
    idf = es.enter_context(nc.sbuf_tensor("idf", [128, 128], F32))
    idb = es.enter_context(nc.sbuf_tensor("idb", [128, 128], BF16))
    onesb = es.enter_context(nc.sbuf_tensor("onesb", [128, 128], BF16))
    onesf = es.enter_context(nc.sbuf_tensor("onesf", [64, 128], F32))
    maskf = es.enter_context(nc.sbuf_tensor("maskf", [64, 64], F32))
    Zcar = es.enter_context(nc.sbuf_tensor("Zcar", [128, 8, 130], F32))
    Ccar = es.enter_context(nc.sbuf_tensor("Ccar", [128, 8, 130], BF16))
    dcar = es.enter_context(nc.sbuf_tensor("dcar", [128, 8], F32))
    acar = es.enter_context(nc.sbuf_tensor("acar", [128, FC, 2], F32))
    Cfin = es.enter_context(nc.sbuf_tensor("Cfin", [128, 8, 130], F32))
    flag_t = es.enter_context(nc.sbuf_tensor("flag_t", [128, 2], F32))
    gdummy = es.enter_context(nc.sbuf_tensor("gdummy", [128, 2], F32))
    AW = min(27500, (nc.sbuf_bytes_remaining - 1024) // 4)
    arena = es.enter_context(nc.sbuf_tensor("arena", [128, AW], F32))
    psb = [es.enter_context(nc.psum_tensor(f"psb{i}", [128, 512], F32)) for i in range(8)]
    psbuf = [Buf() for _ in range(8)]

    class Arena:
        def __init__(self):
            self.off = 0

        def reset(self):
            self.off = 0

        def f32(self, shape):
            shape = list(shape[1:]) if shape[0] == 128 else list(shape)
            n = int(np.prod(shape))
            v = arena[:, self.off:self.off + n]
            self.off += n
            assert self.off <= AW, (self.off, AW)
            if len(shape) == 2:
                return v.rearrange("p (a b) -> p a b", b=shape[1])
            return v

        def bf16(self, shape):
            shape = list(shape[1:]) if shape[0] == 128 else list(shape)
            n = int(np.prod(shape))
            nw = (n + 1) // 2
            v = arena[:, self.off:self.off + nw].bitcast(BF16)[:, 0:n]
            self.off += nw
            assert self.off <= AW, (self.off, AW)
            if len(shape) == 2:
                return v.rearrange("p (a b) -> p a b", b=shape[1])
            return v

    AR = Arena()
    LIVE = []
    guard = {"tok": None}
    gd_buf = Buf()

    def guard_now(keep=()):
        bufs = [b for b in LIVE if b not in keep]
        del LIVE[:]
        tok = P.op("dve", lambda e: e.memset(gdummy[:, 0:1], 0.0), writes=bufs + [gd_buf])
        guard["tok"] = tok
        LIVE.extend(keep)

    def phase_begin():
        guard_now()
        AR.reset()

    def NB():
        b = Buf()
        if guard["tok"] is not None:
            b.w.add(guard["tok"])
        LIVE.append(b)
        return b

    cbuf = Buf()

    def c_load(dst, src, key):
        P.dma("sp", dst, src, key=key, writes=[cbuf], partial=True)

    c_load(vec_t[:], vecs, "c0")
    c_load(bif_t[:], bif, "c1")
    c_load(invc_t[:], invc, "c2")
    c_load(flag_t[:], flags, "c3")
    P.op("pool", lambda e: e.memset(Cfin[:], 0.0), writes=[cbuf], partial=True)
    P.op("pool", lambda e: e.memset(idf[:], 0.0), writes=[cbuf], partial=True)
    P.op("pool", lambda e: e.affine_select(out=idf[:], in_=idf[:], pattern=[[-1, 128]], compare_op=ALU.not_equal,
                                           fill=1.0, base=0, channel_multiplier=1), writes=[cbuf], partial=True)
    P.op("pool", lambda e: e.memset(onesf[:], 1.0), writes=[cbuf], partial=True)
    P.op("pool", lambda e: e.memset(maskf[:], 1.0), writes=[cbuf], partial=True)
    P.op("pool", lambda e: e.affine_select(out=maskf[:], in_=maskf[:], pattern=[[1, 64]], compare_op=ALU.is_ge,
                                           fill=0.0, base=0, channel_multiplier=-1), writes=[cbuf], partial=True)
    P.op("pool", lambda e: e.memset(onesb[:], 1.0), writes=[cbuf], partial=True)
    P.op("dve", lambda e: e.tensor_copy(out=idb[:], in_=idf[:]), reads=[cbuf], writes=[cbuf], partial=True)

    def vcol(l, off, i):
        c = l * cfg.NV + off + i
        return vec_t[:, c:c + 1]

    class WStream:
        def __init__(self):
            self.specs = []
            self.i = 0
            self.issued = 0
            self.slots = [Buf() for _ in range(NSLOT)]

        def _issue(self, k):
            src, nkc, ncols = self.specs[k]
            sl = k % NSLOT
            dst = wring[:, sl, 0:nkc * ncols].rearrange("p (k c) -> p k c", c=ncols)
            P.dma("pool", dst, src, key=f"w{sl}", writes=[self.slots[sl]])

        def get(self, src, nkc, ncols):
            assert nkc * ncols <= 2048
            k = self.i
            self.i += 1
            sl = k % NSLOT
            if P.dry:
                self.specs.append((src.rearrange("(k p) c -> p k c", p=128), nkc, ncols))
                return wring[:, sl, 0:nkc * ncols].rearrange("p (k c) -> p k c", c=ncols), self.slots[sl]
            while self.issued < min(len(self.specs), k + NSLOT - 2):
                self._issue(self.issued)
                self.issued += 1
            sl = k % NSLOT
            return wring[:, sl, 0:nkc * ncols].rearrange("p (k c) -> p k c", c=ncols), self.slots[sl]

    W = WStream()

    ring = {"i": 0}

    def next_bank():
        b = ring["i"] % 4
        ring["i"] += 1
        return b

    hT_buf = Buf()

    def mm_group(bank, ncols, pairs, reads, m=128):
        n = len(pairs)

        def fn(e, pairs=pairs, bank=bank, ncols=ncols, m=m):
            ins = None
            for i, (l, r) in enumerate(pairs):
                ins = e.matmul(psb[bank][0:m, 0:ncols], lhsT=l, rhs=r, start=(i == 0), stop=(i == n - 1))
            return ins

        return P.op("pe", fn, reads=reads, writes=[psbuf[bank]])

    def transpose_in():
        phase_begin()
        xt = [AR.f32([128, D]) for _ in range(4)]
        xo = [AR.f32([KC, 128]) for _ in range(4)]
        xtb = [NB(), NB(), NB(), NB()]
        xob = [NB(), NB(), NB(), NB()]
        tiles = [(0, N_META)] + [(N_META + 128 * i, 128) for i in range((TS - N_META) // 128)]
        for ti, (t0, nt) in enumerate(tiles):
            sl = ti % 4
            P.dma("sp", xt[sl][0:nt, :], xin[t0:t0 + nt, :], key=f"ti{sl}", writes=[xtb[sl]], reads=[])
            for k0 in range(0, KC, 4):
                b = next_bank()
                kn = min(4, KC - k0)

                def fn(e, sl=sl, k0=k0, kn=kn, nt=nt, b=b):
                    ins = None
                    for k in range(kn):
                        ins = e.matmul(psb[b][:, k * 128:k * 128 + nt], lhsT=xt[sl][0:nt, (k0 + k) * 128:(k0 + k + 1) * 128],
                                       rhs=idf[0:nt, 0:nt], start=True, stop=True)
                    return ins

                P.op("pe", fn, reads=[xtb[sl], cbuf], writes=[psbuf[b]])
                P.op("act", lambda e, sl=sl, k0=k0, kn=kn, nt=nt, b=b: e.activation(
                    out=xo[sl][:, k0:k0 + kn, 0:nt], in_=psb[b][:, 0:kn * 128].rearrange("p (k c) -> p k c", c=128)[:, :, 0:nt],
                    func=AF.Copy), reads=[psbuf[b]], writes=[xob[sl]], partial=(k0 > 0))
            P.dma("sp", xT.rearrange("(k p) t -> p k t", p=128)[:, :, t0:t0 + nt], xo[sl][:, :, 0:nt], key=f"to{sl}",
                  reads=[xob[sl]], writes=[xT_buf], partial=True)

    def transpose_out():
        phase_begin()
        xi = [AR.f32([KC, 128]) for _ in range(4)]
        xo = [AR.f32([128, D]) for _ in range(4)]
        xib = [NB(), NB(), NB(), NB()]
        xob = [NB(), NB(), NB(), NB()]
        ntile = (TS - N_META) // 128
        toks = []
        for ti in range(ntile):
            t0 = N_META + 128 * ti
            sl = ti % 4
            P.dma("sp", xi[sl][:], xT.rearrange("(k p) t -> p k t", p=128)[:, :, t0:t0 + 128], key=f"ti{sl}",
                  reads=[xT_buf], writes=[xib[sl]])
            for k0 in range(0, KC, 4):
                b = next_bank()
                kn = min(4, KC - k0)

                def fn(e, sl=sl, k0=k0, kn=kn, b=b):
                    ins = None
                    for k in range(kn):
                        ins = e.matmul(psb[b][:, k * 128:(k + 1) * 128], lhsT=xi[sl][:, k0 + k, :], rhs=idf[:],
                                       start=True, stop=True)
                    return ins

                P.op("pe", fn, reads=[xib[sl], cbuf], writes=[psbuf[b]])
                P.op("act", lambda e, sl=sl, k0=k0, kn=kn, b=b: e.activation(
                    out=xo[sl][:, k0 * 128:(k0 + kn) * 128], in_=psb[b][:, 0:kn * 128], func=AF.Copy),
                    reads=[psbuf[b]], writes=[xob[sl]], partial=(k0 > 0))
            toks.append(P.dma("sp", out[128 * ti:128 * ti + 128, :], xo[sl][:], key=f"to{sl}", reads=[xob[sl]]))
        P.op("sp", lambda e: e.nop(), deps=toks[-4:])

    xT_buf = Buf()
    ksc_buf = Buf(); cfin_buf = Buf(); sendc_buf = Buf(); recvc_buf = Buf(); sendx_buf = Buf(); recvx_buf = Buf()
    oT_buf = Buf()
    ysc_buf = Buf()
    gsc_buf = Buf()

    def sumsq_rstd(src3, ncols, rstd, sq, sqb, rsb, srcbuf):
        P.op("act", lambda e: e.activation(out=sq[:, :, 0:ncols], in_=src3[:, :, 0:ncols], func=AF.Square),
             reads=[srcbuf], writes=[sqb])
        b = next_bank()
        mm_group(b, ncols, [(onesb[:], sq[:, k, 0:ncols]) for k in range(KC)], reads=[sqb, cbuf])
        P.op("act", lambda e: e.activation(out=rstd[:, 0:ncols], in_=psb[b][:, 0:ncols], func=AF.Ln, scale=1.0 / D,
                                           bias=EPS), reads=[psbuf[b]], writes=[rsb])
        P.op("act", lambda e: e.activation(out=rstd[:, 0:ncols], in_=rstd[:, 0:ncols], func=AF.Exp, scale=-0.5),
             reads=[rsb], writes=[rsb])

    def norm_pass(l, voff, g0, lo, n):
        phase_begin()
        PW = 344
        xp = [AR.f32([KC, PW]) for _ in range(2)]
        sq = AR.bf16([KC, PW])
        rstd = AR.f32([128, PW])
        xpb = [NB(), NB()]
        sqb, rsb = NB(), NB()
        pcs = even_split(n, PW)
        for pi, (c0, w) in enumerate(pcs):
            sl = pi % 2
            P.dma("sp", xp[sl][:, :, 0:w], xT.rearrange("(k p) t -> p k t", p=128)[:, :, g0 + lo + c0:g0 + lo + c0 + w],
                  key=f"np{sl}", reads=[xT_buf], writes=[xpb[sl]])
            sumsq_rstd(xp[sl], w, rstd, sq, sqb, rsb, xpb[sl])
            for k in range(KC):
                P.op("dve", lambda e, k=k, sl=sl, w=w, c0=c0: e.scalar_tensor_tensor(
                    out=hTt[:, k, lo + c0:lo + c0 + w], in0=xp[sl][:, k, 0:w], scalar=vcol(l, voff, k), in1=rstd[:, 0:w],
                    op0=ALU.mult, op1=ALU.mult), reads=[xpb[sl], rsb, cbuf], writes=[hT_buf], partial=True)

    def update_pass(l, voff, g0, lo, n, nxt=None):
        phase_begin()
        PW = 344
        xp = [AR.f32([KC, PW]) for _ in range(2)]
        op_ = [AR.f32([KC, PW]) for _ in range(2)]
        sq = AR.bf16([KC, PW])
        rstd = AR.f32([128, PW])
        rstd2 = AR.f32([128, PW])
        xpb = [NB(), NB()]
        opb = [NB(), NB()]
        sqb, rsb, rsb2 = NB(), NB(), NB()
        pcs = even_split(n, PW)
        xv = xT.rearrange("(k p) t -> p k t", p=128)
        ov = oT.rearrange("(k p) t -> p k t", p=128)

        def loads(pi):
            c0, w = pcs[pi]
            sl = pi % 2
            a, b_ = g0 + lo + c0, g0 + lo + c0 + w
            P.dma("sp", op_[sl][:, :, 0:w], ov[:, :, a:b_], key=f"up{sl}", reads=[oT_buf], writes=[opb[sl]])
            P.dma("sp", xp[sl][:, :, 0:w], xv[:, :, a:b_], key=f"np{sl}", reads=[xT_buf], writes=[xpb[sl]])

        loads(0)
        for pi, (c0, w) in enumerate(pcs):
            sl = pi % 2
            a, b_ = g0 + lo + c0, g0 + lo + c0 + w
            if pi + 1 < len(pcs):
                loads(pi + 1)
            sumsq_rstd(op_[sl], w, rstd, sq, sqb, rsb, opb[sl])
            for k in range(KC):
                P.op("dve", lambda e, k=k, w=w, sl=sl: e.scalar_tensor_tensor(
                    out=op_[sl][:, k, 0:w], in0=op_[sl][:, k, 0:w], scalar=vcol(l, voff, k), in1=rstd[:, 0:w],
                    op0=ALU.mult, op1=ALU.mult), reads=[rsb, cbuf], writes=[opb[sl]], partial=True)
                P.op("pool", lambda e, k=k, w=w, sl=sl: e.tensor_tensor(
                    out=xp[sl][:, k, 0:w], in0=xp[sl][:, k, 0:w], in1=op_[sl][:, k, 0:w], op=ALU.add),
                    reads=[opb[sl]], writes=[xpb[sl]], partial=True)
            P.dma("sp", xv[:, :, a:b_], xp[sl][:, :, 0:w], key=f"ux{sl}", reads=[xpb[sl]], writes=[xT_buf], partial=True)
            if nxt is not None:
                l2, voff2 = nxt
                sumsq_rstd(xp[sl], w, rstd2, sq, sqb, rsb2, xpb[sl])
                for k in range(KC):
                    P.op("dve", lambda e, k=k, sl=sl, w=w, c0=c0: e.scalar_tensor_tensor(
                        out=hTt[:, k, lo + c0:lo + c0 + w], in0=xp[sl][:, k, 0:w], scalar=vcol(l2, voff2, k),
                        in1=rstd2[:, 0:w], op0=ALU.mult, op1=ALU.mult), reads=[xpb[sl], rsb2, cbuf], writes=[hT_buf],
                        partial=True)

    def mixer_p1(l, s):
        g0 = s * NT
        lo = 0 if s == 0 else N_META
        phase_begin()
        pcs = even_split(TC, 512)
        chunks = [(0, 0, N_META)] + [(c, N_META + CH * (c - 1), CH) for c in range(1, NCH)]
        act_chunks = chunks if s == 0 else chunks[1:]
        wl = w_in[l]
        Yst = [AR.bf16([128, TC]) for _ in range(2)]; Ystb = [NB(), NB()]
        mark = AR.off
        Gall = AR.f32([NCH, 16]); LF = AR.f32([NCH, 8]); Wall = AR.f32([NCH, 8]); EB = AR.f32([NCH, 8])
        DEC = AR.f32([NCH, 8])
        tmp8 = [AR.f32([128, 8]) for _ in range(2)]
        gb = NB()
        mark2 = AR.off

        wg, wgb = W.get(wl[:, 4 * MIXW:4 * MIXW + 16], KC, 16)
        for (c, c0, nt) in act_chunks:
            def fn(e, c0=c0, nt=nt):
                ins = None
                for k in range(KC):
                    ins = e.matmul(psb[6][0:nt, 0:16], lhsT=hTt[:, k, c0:c0 + nt], rhs=wg[:, k, :], start=(k == 0),
                                   stop=(k == KC - 1))
                return ins

            P.op("pe", fn, reads=[hT_buf, wgb], writes=[psbuf[6]])
            P.op("dve", lambda e, c=c, nt=nt: e.tensor_tensor(out=Gall[0:nt, c, :], in0=psb[6][0:nt, 0:16],
                                                              in1=bif_t[0:nt, l * 16:l * 16 + 16], op=ALU.add),
                 reads=[psbuf[6], cbuf], writes=[gb], partial=True)
            P.op("act", lambda e, c=c, nt=nt: e.activation(out=tmp8[0][0:nt, :], in_=Gall[0:nt, c, 8:16], func=AF.Exp,
                                                           scale=-1.0), reads=[gb], writes=[gb], partial=True)
            P.op("act", lambda e, c=c, nt=nt: e.activation(out=LF[0:nt, c, :], in_=tmp8[0][0:nt, :], func=AF.Ln,
                                                           bias=1.0), reads=[gb], writes=[gb], partial=True)
            if c == 0:
                P.op("dve", lambda e, nt=nt: e.tensor_scalar(out=LF[0:nt, 0, :], in0=LF[0:nt, 0, :],
                                                            scalar1=flag_t[0:nt, 0:1], scalar2=None, op0=ALU.mult),
                     reads=[gb, cbuf], writes=[gb], partial=True)
            P.op("pe", lambda e, c=c, nt=nt: e.matmul(psb[7][0:nt, 0:8], lhsT=maskf[0:nt, 0:nt], rhs=LF[0:nt, c, :],
                                                      start=True, stop=True), reads=[gb, cbuf], writes=[psbuf[7]])
            P.op("pe", lambda e, c=c, nt=nt: e.matmul(psb[5][:, 0:8], lhsT=onesf[0:nt, :], rhs=LF[0:nt, c, :],
                                                      start=True, stop=True), reads=[gb, cbuf], writes=[psbuf[5]])
            P.op("act", lambda e, c=c: e.activation(out=DEC[:, c, :], in_=psb[5][:, 0:8], func=AF.Exp, scale=-1.0),
                 reads=[psbuf[5]], writes=[gb], partial=True)
            P.op("dve", lambda e, c=c, nt=nt: e.tensor_tensor(out=tmp8[1][0:nt, :], in0=Gall[0:nt, c, 0:8],
                                                              in1=psb[7][0:nt, 0:8], op=ALU.add),
                 reads=[psbuf[7], gb], writes=[gb], partial=True)
            P.op("act", lambda e, c=c, nt=nt: e.activation(out=Wall[0:nt, c, :], in_=tmp8[1][0:nt, :], func=AF.Exp),
                 reads=[gb], writes=[gb], partial=True)
            if c == 0:
                P.op("dve", lambda e, nt=nt: e.tensor_scalar(out=Wall[0:nt, 0, :], in0=Wall[0:nt, 0, :],
                                                            scalar1=flag_t[0:nt, 0:1], scalar2=None, op0=ALU.mult),
                     reads=[gb, cbuf], writes=[gb], partial=True)
            P.op("act", lambda e, c=c, nt=nt: e.activation(out=EB[0:nt, c, :], in_=psb[7][0:nt, 0:8], func=AF.Exp,
                                                           scale=-1.0, bias=-0.5 * math.log(DH)),
                 reads=[psbuf[7]], writes=[gb], partial=True)

        def rot(banks):
            st = {"i": 0}

            def nxt():
                b = banks[st["i"] % len(banks)]
                st["i"] += 1
                return b
            return nxt

        ring_sw = rot([0, 1])

        def gen_sweep(col0, evac):
            wt, wb = W.get(wl[:, col0:col0 + 128], KC, 128)
            for (c0, w) in pcs:
                b = ring_sw()
                mm_group(b, w, [(wt[:, k, :], hTt[:, k, c0:c0 + w]) for k in range(KC)], reads=[hT_buf, wb])
                evac(b, c0, w)
                yield

        def chain(*gens):
            for g in gens:
                for _ in g:
                    yield

        def interleave(a, b, every):
            i = 0
            b_done = b is None
            for _ in a:
                i += 1
                if not b_done and i % every == 0:
                    try:
                        next(b)
                    except StopIteration:
                        b_done = True
            if not b_done:
                for _ in b:
                    pass

        def sweep_chunk(col0, evac, nkc=KC, src=None, rhs_of=None):
            wt, wb = W.get((wl if src is None else src)[:, col0:col0 + 128], nkc, 128)
            for (c0, w) in pcs:
                b = next_bank()
                mm_group(b, w, [(wt[:, k, :], hTt[:, k, c0:c0 + w]) for k in range(nkc)], reads=[hT_buf, wb])
                evac(b, c0, w)

        def ev_copy(dst, dbuf, eng):
            def f(b, c0, w):
                if eng == "act":
                    P.op("act", lambda e: e.activation(out=dst[:, c0:c0 + w], in_=psb[b][:, 0:w], func=AF.Copy),
                         reads=[psbuf[b]], writes=[dbuf], partial=(c0 > 0))
                else:
                    P.op("dve", lambda e: e.tensor_copy(out=dst[:, c0:c0 + w], in_=psb[b][:, 0:w]),
                         reads=[psbuf[b]], writes=[dbuf], partial=(c0 > 0))
            return f

        def ev_sig(dst, dbuf):
            def f(b, c0, w):
                P.op("act", lambda e: e.activation(out=dst[:, c0:c0 + w], in_=psb[b][:, 0:w], func=AF.Sigmoid),
                     reads=[psbuf[b]], writes=[dbuf], partial=(c0 > 0))
            return f

        KV = [[AR.bf16([128, TC]) for _ in range(2)] for _ in range(2)]
        KVb = [[NB() for _ in range(2)] for _ in range(2)]
        Kt1 = [AR.bf16([NCH, 128]) for _ in range(2)]; Kt1b = [[NB() for _ in range(NCH)] for _ in range(2)]
        Vp1 = [AR.bf16([NCH, 130]) for _ in range(2)]; Vp1b = [[NB() for _ in range(NCH)] for _ in range(2)]
        Vp1o = [NB(), NB()]
        Z1 = [AR.f32([128, 130]) for _ in range(2)]
        for i_ in range(2):
            P.op("dve", lambda e, i_=i_: e.memset(Vp1[i_][:, :, :], 0.0), writes=[Vp1o[i_]] + Vp1b[i_])

        r_tk, r_tv, r_u1 = rot([2, 3]), rot([4, 5]), rot([6, 7])

        def sweeps1(j):
            qs = j % 2
            k_, v_ = KV[qs]
            kb_, vb_ = KVb[qs]
            for _ in gen_sweep(1 * MIXW + j * 128, ev_copy(k_, kb_, "act")):
                yield
            for _ in gen_sweep(2 * MIXW + j * 128, ev_copy(v_, vb_, "act")):
                yield
            P.dma("sp", ksc[j], k_[:, :], key=f"ks{qs}", reads=[kb_], writes=[ksc_buf], partial=True)

        def chunks1(j):
            qs = j % 2
            k_, v_ = KV[qs]
            kb_, vb_ = KVb[qs]
            Ktok, Ktc, Vp, Vpc, Vpo = Kt1[qs], Kt1b[qs], Vp1[qs], Vp1b[qs], Vp1o[qs]
            P.op("dve", lambda e: e.tensor_copy(out=Vp[0:64, :, 128], in_=Wall[0:64, :, j]), reads=[gb], writes=[Vpo] + Vpc)
            st = {"zprev": None, "dprev": None, "zpb": None, "ci": 0}
            zbufs = [NB(), NB()]

            def emit_u(c, c0, nt):
                bu = r_u1()
                ci = st["ci"]
                st["ci"] += 1
                P.op("pe", lambda e: e.matmul(psb[bu][:, 0:129], lhsT=Ktok[0:nt, c, :], rhs=Vp[0:nt, c, 0:129], start=True,
                                              stop=True), reads=[Ktc[c], Vpc[c], Vpo], writes=[psbuf[bu]])
                zn = Z1[ci % 2][:, 0:129]
                zprev, dprev, zpb = st["zprev"], st["dprev"], st["zpb"]
                if zprev is None:
                    P.op("dve", lambda e: e.tensor_copy(out=zn, in_=psb[bu][:, 0:129]), reads=[psbuf[bu]],
                         writes=[zbufs[ci % 2]])
                else:
                    P.op("dve", lambda e: e.scalar_tensor_tensor(out=zn, in0=zprev, scalar=dprev, in1=psb[bu][:, 0:129],
                                                                 op0=ALU.mult, op1=ALU.add),
                         reads=[psbuf[bu], zpb, gb], writes=[zbufs[ci % 2]])
                st["zprev"], st["dprev"], st["zpb"] = zn, DEC[:, c, j:j + 1], zbufs[ci % 2]

            pend = None
            for (c, c0, nt) in act_chunks:
                bk, bv = r_tk(), r_tv()
                P.op("pe", lambda e, c0=c0, nt=nt, bk=bk: e.matmul(psb[bk][0:nt, 0:128], lhsT=k_[:, c0:c0 + nt], rhs=idb[:],
                                                                   start=True, stop=True), reads=[kb_, cbuf],
                     writes=[psbuf[bk]])
                P.op("dve", lambda e, c=c, nt=nt, bk=bk: e.tensor_copy(out=Ktok[0:nt, c, :], in_=psb[bk][0:nt, 0:128]),
                     reads=[psbuf[bk]], writes=[Ktc[c]])
                P.op("pe", lambda e, c0=c0, nt=nt, bv=bv: e.matmul(psb[bv][0:nt, 0:128], lhsT=v_[:, c0:c0 + nt], rhs=idb[:],
                                                                   start=True, stop=True), reads=[vb_, cbuf],
                     writes=[psbuf[bv]])
                P.op("dve", lambda e, c=c, nt=nt, bv=bv: e.tensor_scalar(out=Vp[0:nt, c, 0:128], in0=psb[bv][0:nt, 0:128],
                                                                         scalar1=Wall[0:nt, c, j:j + 1], scalar2=None,
                                                                         op0=ALU.mult),
                     reads=[psbuf[bv], gb, Vpo], writes=[Vpc[c]], partial=True)
                if pend is not None:
                    emit_u(*pend)
                pend = (c, c0, nt)
                yield
            emit_u(*pend)
            P.dma("sp", ktsc[j], Ktok[0:64, :, :], key=f"kt{qs}", reads=Ktc, writes=[ksc_buf], partial=True)
            P.dma("sp", vpsc[j], Vp[0:64, :, :], key=f"vp{qs}", reads=Vpc + [Vpo], writes=[ksc_buf], partial=True)
            zprev, dprev, zpb = st["zprev"], st["dprev"], st["zpb"]
            P.op("dve", lambda e: e.tensor_scalar(out=Cfin[:, j, 0:129], in0=zprev, scalar1=dprev, scalar2=None,
                                                  op0=ALU.mult),
                 reads=[zpb, gb], writes=[cfin_buf], partial=(j > 0))

        for _ in sweeps1(0):
            pass
        for j in range(NH):
            interleave(chunks1(j), sweeps1(j + 1) if j + 1 < NH else None, 3)

        P.dma("sp", send_c.rearrange("(h p) c -> p h c", p=128), Cfin[:, :, :], key="xc0", reads=[cfin_buf],
              writes=[sendc_buf])
        P.coll(send_c, recv_c, GROUPS, reads=[sendc_buf], writes=[recvc_buf])
        P.dma("sp", Zcar[:, :, :], recv_c[0:NH * 128, :].rearrange("(h p) c -> p h c", p=128), key="xc1",
              reads=[recvc_buf], writes=[car_buf])
        P.op("dve", lambda e: e.tensor_scalar(out=Zcar[:, :, :], in0=Zcar[:, :, :], scalar1=flag_t[:, 1:2], scalar2=None,
                                              op0=ALU.mult), reads=[car_buf, cbuf], writes=[car_buf], partial=True)
        P.op("dve", lambda e: e.tensor_copy(out=Ccar[:, :, :], in_=Zcar[:, :, :]), reads=[car_buf], writes=[car_buf],
             partial=True)
        P.op("dve", lambda e: e.memset(dcar[:], 1.0), writes=[car_buf], partial=True)

        guard_now(keep=Ystb + [gb])
        AR.off = mark2
        QO = [[AR.bf16([128, TC]) for _ in range(2)] for _ in range(2)]
        QOb = [[NB() for _ in range(2)] for _ in range(2)]
        K2 = [AR.bf16([128, TC]) for _ in range(2)]; K2b = [NB(), NB()]
        Ktok = AR.bf16([NCH, 128]); Ktb = NB()
        Vp = AR.bf16([NCH, 130]); Vpb = NB()
        Cbf = AR.bf16([NCH, 130]); Cbb = [NB() for _ in range(NCH)]
        Hs = AR.f32([NCH, 130]); Hsb = NB()
        Hn = AR.bf16([NCH, 128]); Hnc = [NB() for _ in range(NCH)]
        Sm = AR.bf16([NCH, 64]); Smc = [NB() for _ in range(NCH)]
        SS = AR.f32([128, NCH]); Zb = [AR.f32([128, 130]) for _ in range(2)]
        t64 = [AR.f32([128, NCH]) for _ in range(4)]
        junk = AR.f32([128, 128])

        r_u2, r_s2, r_h2 = rot([2, 3]), rot([4, 5]), rot([6, 7])

        def sweeps2(j):
            qs = j % 2
            q_, o_ = QO[qs]
            qb_, ob_ = QOb[qs]
            k_, kb_ = K2[qs], K2b[qs]
            P.dma("sp", k_[:, :], ksc[j], key=f"k2{qs}", reads=[ksc_buf], writes=[kb_])
            for _ in gen_sweep(0 * MIXW + j * 128, ev_copy(q_, qb_, "act")):
                yield
            for _ in gen_sweep(3 * MIXW + j * 128, ev_sig(o_, ob_)):
                yield

        def chunks2(j):
            qs = j % 2
            q_, o_ = QO[qs]
            qb_, ob_ = QOb[qs]
            k_, kb_ = K2[qs], K2b[qs]
            if j == 0:
                P.dma("sp", Ktok[0:64, :, :], ktsc[j], key="kt2", reads=[ksc_buf], writes=[Ktb])
                P.dma("sp", Vp[0:64, :, :], vpsc[j], key="vp2", reads=[ksc_buf], writes=[Vpb])
            P.op("dve", lambda e: e.memset(SS[:, :], 0.0), reads=[Hsb], writes=[Hsb])
            zprev = Zcar[:, j, 0:129]
            dprev = dcar[:, j:j + 1]
            zpb = NB()
            zpb.w.update(car_buf.w)
            zbufs = [NB(), NB()]
            cprev = Ccar[:, j, 0:129]
            cpb = car_buf
            pend = None

            def emit_h(c, c0, nt, cprev, cpb):
                bh = r_h2()

                def fn(e):
                    e.matmul(psb[bh][0:nt, 0:129], lhsT=q_[:, c0:c0 + nt], rhs=cprev, start=True, stop=False)
                    return e.matmul(psb[bh][0:nt, 0:129], lhsT=Sm[0:nt, c, 0:nt], rhs=Vp[0:nt, c, 0:129], start=False,
                                    stop=True)

                P.op("pe", fn, reads=[qb_, cpb, Smc[c], Vpb], writes=[psbuf[bh]])
                P.op("act", lambda e: e.activation(out=Hs[0:nt, c, 0:129], in_=psb[bh][0:nt, 0:129], func=AF.Copy),
                     reads=[psbuf[bh]], writes=[Hsb], partial=True)
                P.op("act", lambda e: e.activation(out=junk[0:nt, :], in_=psb[bh][0:nt, 0:128], func=AF.Square,
                                                   accum_out=SS[0:nt, c:c + 1]),
                     reads=[psbuf[bh]], writes=[Hsb], partial=True)

            for ci, (c, c0, nt) in enumerate(act_chunks):
                bu, bs = r_u2(), r_s2()
                P.op("pe", lambda e, c=c, nt=nt, bu=bu: e.matmul(psb[bu][:, 0:129], lhsT=Ktok[0:nt, c, :],
                                                                 rhs=Vp[0:nt, c, 0:129], start=True, stop=True),
                     reads=[Ktb, Vpb], writes=[psbuf[bu]])
                zn = Zb[ci % 2][:, 0:129]
                P.op("dve", lambda e, zn=zn, zprev=zprev, dprev=dprev, bu=bu: e.scalar_tensor_tensor(
                    out=zn, in0=zprev, scalar=dprev, in1=psb[bu][:, 0:129], op0=ALU.mult, op1=ALU.add),
                    reads=[psbuf[bu], zpb, gb], writes=[zbufs[ci % 2]])
                P.op("act", lambda e, zn=zn, c=c: e.activation(out=Cbf[:, c, 0:129], in_=zn, func=AF.Copy,
                                                               scale=DEC[:, c, j:j + 1]),
                     reads=[zbufs[ci % 2], gb], writes=[Cbb[c]])
                zprev, dprev, zpb = zn, DEC[:, c, j:j + 1], zbufs[ci % 2]
                P.op("pe", lambda e, c0=c0, nt=nt, bs=bs: e.matmul(psb[bs][0:nt, 0:nt], lhsT=k_[:, c0:c0 + nt],
                                                                   rhs=q_[:, c0:c0 + nt], start=True, stop=True),
                     reads=[kb_, qb_], writes=[psbuf[bs]])
                P.op("dve", lambda e, c=c, nt=nt, bs=bs: e.tensor_tensor(out=Sm[0:nt, c, 0:nt], in0=psb[bs][0:nt, 0:nt],
                                                                         in1=maskf[0:nt, 0:nt], op=ALU.mult),
                     reads=[psbuf[bs], cbuf], writes=[Smc[c]])
                if pend is not None:
                    emit_h(*pend)
                pend = (c, c0, nt, cprev, cpb)
                cprev, cpb = Cbf[:, c, 0:129], Cbb[c]
                yield
            emit_h(*pend)
            if j + 1 < NH:
                P.dma("sp", Ktok[0:64, :, :], ktsc[j + 1], key="kt2", reads=[ksc_buf], writes=[Ktb])
                P.dma("sp", Vp[0:64, :, :], vpsc[j + 1], key="vp2", reads=[ksc_buf], writes=[Vpb])
            yield
            R = slice(0, NCH)
            ebj = EB[0:64, R, j]
            T0, T1, T2, T3 = [t[0:64, R] for t in t64]
            ssr = SS[0:64, R]
            P.op("dve", lambda e: e.tensor_tensor(out=T0, in0=Hs[0:64, R, 128], in1=ebj, op=ALU.mult),
                 reads=[Hsb, gb], writes=[Hsb], partial=True)
            P.op("act", lambda e: e.activation(out=T0, in_=T0, func=AF.Abs), reads=[Hsb], writes=[Hsb], partial=True)
            P.op("dve", lambda e: e.tensor_single_scalar(out=T0, in_=T0, scalar=1.0, op=ALU.max),
                 reads=[Hsb], writes=[Hsb], partial=True)
            P.op("dve", lambda e: e.reciprocal(out=T0, in_=T0), reads=[Hsb], writes=[Hsb], partial=True)
            P.op("dve", lambda e: e.tensor_tensor(out=T1, in0=ebj, in1=T0, op=ALU.mult),
                 reads=[Hsb], writes=[Hsb], partial=True)
            yield
            P.op("dve", lambda e: e.tensor_tensor(out=T2, in0=T1, in1=T1, op=ALU.mult),
                 reads=[Hsb], writes=[Hsb], partial=True)
            P.op("dve", lambda e: e.tensor_tensor(out=T2, in0=T2, in1=ssr, op=ALU.mult),
                 reads=[Hsb], writes=[Hsb], partial=True)
            yield
            P.op("act", lambda e: e.activation(out=T2, in_=T2, func=AF.Ln, scale=1.0 / DH, bias=EPS),
                 reads=[Hsb], writes=[Hsb], partial=True)
            P.op("act", lambda e: e.activation(out=T2, in_=T2, func=AF.Exp, scale=-0.5),
                 reads=[Hsb], writes=[Hsb], partial=True)
            P.op("dve", lambda e: e.tensor_tensor(out=T3, in0=T1, in1=T2, op=ALU.mult),
                 reads=[Hsb], writes=[Hsb], partial=True)
            ys = j % 2

            def emit_hn(c, c0, nt):
                P.op("dve", lambda e: e.tensor_scalar(out=Hn[0:nt, c, :], in0=Hs[0:nt, c, 0:128],
                                                      scalar1=t64[3][0:nt, c:c + 1], scalar2=None, op0=ALU.mult),
                     reads=[Hsb], writes=[Hnc[c]])

            emit_hn(*act_chunks[0])
            for ci, (c, c0, nt) in enumerate(act_chunks):
                bt = r_u2()
                if ci + 1 < len(act_chunks):
                    emit_hn(*act_chunks[ci + 1])
                P.op("pe", lambda e, c=c, nt=nt, bt=bt: e.matmul(psb[bt][:, 0:nt], lhsT=Hn[0:nt, c, :], rhs=idb[0:nt, 0:nt],
                                                                 start=True, stop=True), reads=[Hnc[c], cbuf],
                     writes=[psbuf[bt]])
                P.op("dve", lambda e, c0=c0, nt=nt, bt=bt: e.scalar_tensor_tensor(
                    out=Yst[ys][:, c0:c0 + nt], in0=psb[bt][:, 0:nt], scalar=vcol(l, cfg.v_gain, j),
                    in1=o_[:, c0:c0 + nt], op0=ALU.mult, op1=ALU.mult),
                    reads=[psbuf[bt], ob_, cbuf], writes=[Ystb[ys]], partial=True)
                yield
            P.dma("sp", ysc[j * 128:(j + 1) * 128, g0 + lo:g0 + TC], Yst[ys][:, lo:TC], key=f"ys{ys}",
                  reads=[Ystb[ys]], writes=[ysc_buf], partial=True)

        for _ in sweeps2(0):
            pass
        for j in range(NH):
            interleave(chunks2(j), sweeps2(j + 1) if j + 1 < NH else None, 6)

        guard_now(keep=Ystb)
        AR.off = mark
        CB = AR.f32([128, TC]); CC = AR.f32([128, TC]); PR = AR.f32([128, TC + 2]); T1c = AR.f32([128, TC])
        cbb, ccb, prb, t1b = NB(), NB(), NB(), NB()
        P.op("dve", lambda e: e.memset(PR[:, 0:2], 0.0), writes=[prb])
        for c in range(8):
            def ev_cc(b, c0, w):
                P.op("act", lambda e: e.activation(out=CC[:, c0:c0 + w], in_=psb[b][:, 0:w], func=AF.Copy),
                     reads=[psbuf[b]], writes=[ccb], partial=(c0 > 0))

            def ev_cx(b, c0, w):
                P.op("dve", lambda e: e.tensor_tensor(out=PR[:, 2 + c0:2 + c0 + w], in0=psb[b][:, 0:w],
                                                      in1=CC[:, c0:c0 + w], op=ALU.mult),
                     reads=[psbuf[b], ccb], writes=[prb], partial=True)

            def ev_cb(b, c0, w):
                P.op("act", lambda e: e.activation(out=CB[:, c0:c0 + w], in_=psb[b][:, 0:w], func=AF.Copy),
                     reads=[psbuf[b]], writes=[cbb], partial=(c0 > 0))

            sweep_chunk(4 * MIXW + 16 + 1 * MIXW + c * 128, ev_cc)
            sweep_chunk(4 * MIXW + 16 + 2 * MIXW + c * 128, ev_cx)
            sweep_chunk(4 * MIXW + 16 + 0 * MIXW + c * 128, ev_cb)
            w0, w1, w2 = [vcol(l, cfg.v_convw, jj * 8 + c) for jj in range(3)]
            P.op("dve", lambda e, w0=w0: e.tensor_scalar(out=T1c[:, :], in0=PR[:, 0:TC], scalar1=w0, scalar2=None,
                                                         op0=ALU.mult), reads=[prb, cbuf], writes=[t1b])
            P.op("dve", lambda e, w1=w1: e.scalar_tensor_tensor(out=T1c[:, :], in0=PR[:, 1:TC + 1], scalar=w1,
                                                                in1=T1c[:, :], op0=ALU.mult, op1=ALU.add),
                 reads=[prb, cbuf], writes=[t1b], partial=True)
            P.op("dve", lambda e, w2=w2: e.scalar_tensor_tensor(out=T1c[:, :], in0=PR[:, 2:TC + 2], scalar=w2,
                                                                in1=T1c[:, :], op0=ALU.mult, op1=ALU.add),
                 reads=[prb, cbuf], writes=[t1b], partial=True)
            ys = c % 2
            P.op("dve", lambda e, ys=ys: e.tensor_tensor(out=Yst[ys][:, :], in0=T1c[:, :], in1=CB[:, :], op=ALU.mult),
                 reads=[t1b, cbb], writes=[Ystb[ys]])
            P.dma("sp", ysc[MIXW + c * 128:MIXW + (c + 1) * 128, g0 + lo:g0 + TC], Yst[ys][:, lo:TC], key=f"ys{ys}",
                  reads=[Ystb[ys]], writes=[ysc_buf], partial=True)

        guard_now(keep=Ystb)
        AR.off = mark
        PAD = 16
        L = PAD + TC
        X = [AR.f32([128, L]) for _ in range(5)]
        Xb = [NB() for _ in range(5)]
        Pp = [AR.bf16([128, TC]) for _ in range(2)]
        Ppb = [NB(), NB()]
        P.op("dve", lambda e: e.memset(X[0][:, 0:PAD], 0.0), writes=[Xb[0]])
        for c in range(8):
            g = c // 2
            def ev_pu(b, c0, w):
                P.op("act", lambda e: e.activation(out=X[0][:, PAD + c0:PAD + c0 + w], in_=psb[b][:, 0:w], func=AF.Copy),
                     reads=[psbuf[b]], writes=[Xb[0]], partial=True)
            sweep_chunk(4 * MIXW + 16 + 3 * MIXW + c * 128, ev_pu)
            for lv in range(1, g + 2):
                sh = 1 << (lv - 1)
                st = (1 << lv) - 1
                P.op("dve", lambda e, lv=lv, sh=sh, st=st: e.tensor_tensor(out=X[lv][:, st:L], in0=X[lv - 1][:, st:L],
                                                                          in1=X[lv - 1][:, st - sh:L - sh], op=ALU.add),
                     reads=[Xb[lv - 1]], writes=[Xb[lv]])
            top = g + 1
            wdw = POOL_W[g]
            pp = Pp[c % 2]
            P.op("dve", lambda e, top=top, wdw=wdw, pp=pp: e.scalar_tensor_tensor(
                out=pp[:, N_META:TC], in0=X[top][:, PAD + N_META:L], scalar=1.0 / wdw, in1=X[0][:, PAD + N_META:L],
                op0=ALU.mult, op1=ALU.subtract), reads=[Xb[top], Xb[0]], writes=[Ppb[c % 2]])
            if s == 0:
                P.op("dve", lambda e, top=top, g=g: e.tensor_tensor(out=X[top][:, PAD:PAD + N_META],
                                                                    in0=X[top][:, PAD:PAD + N_META],
                                                                    in1=invc_t[:, g * 16:(g + 1) * 16], op=ALU.mult),
                     reads=[Xb[top], cbuf, Ppb[c % 2]], writes=[Xb[top]], partial=True)
                P.op("dve", lambda e, top=top, pp=pp: e.tensor_tensor(out=pp[:, 0:N_META], in0=X[top][:, PAD:PAD + N_META],
                                                                      in1=X[0][:, PAD:PAD + N_META], op=ALU.subtract),
                     reads=[Xb[top], Xb[0]], writes=[Ppb[c % 2]], partial=True)
            if c % 2 == 1:
                pw, pwb = W.get(pool_w[l][g * 256:(g + 1) * 256, :], 2, 256)
                for dch in range(2):
                    ys = dch
                    for (c0, w) in pcs:
                        b = next_bank()
                        mm_group(b, w, [(pw[:, k, dch * 128:(dch + 1) * 128], Pp[k][:, c0:c0 + w]) for k in range(2)],
                                 reads=[Ppb[0], Ppb[1], pwb])
                        P.op("act", lambda e, b=b, c0=c0, w=w, ys=ys, g=g, dch=dch: e.activation(
                            out=Yst[ys][:, c0:c0 + w], in_=psb[b][:, 0:w], func=AF.Copy,
                            scale=vcol(l, cfg.v_pscale, g * 2 + dch)),
                            reads=[psbuf[b], cbuf], writes=[Ystb[ys]], partial=(c0 > 0))
                    r0 = 2 * MIXW + g * 256 + dch * 128
                    P.dma("sp", ysc[r0:r0 + 128, g0 + lo:g0 + TC], Yst[ys][:, lo:TC], key=f"ys{ys}",
                          reads=[Ystb[ys]], writes=[ysc_buf], partial=True)

        for m in range(3 * KC):
            ys = m % 2

            def ev_g(b, c0, w, ys=ys):
                P.op("act", lambda e: e.activation(out=Yst[ys][:, c0:c0 + w], in_=psb[b][:, 0:w], func=AF.Sigmoid),
                     reads=[psbuf[b]], writes=[Ystb[ys]], partial=(c0 > 0))

            sweep_chunk(8 * MIXW + 16 + m * 128, ev_g)
            P.dma("sp", gsc[m * 128:(m + 1) * 128, g0 + lo:g0 + TC], Yst[ys][:, lo:TC], key=f"ys{ys}",
                  reads=[Ystb[ys]], writes=[gsc_buf], partial=True)

    car_buf = Buf()

    def mixer_p2(l, g0, lo, n):
        phase_begin()
        pcs = even_split(n, 512)
        Y = AR.bf16([24, n]); Yb = NB()
        Gt = [AR.bf16([3, n]) for _ in range(2)]; Gtb = [NB(), NB()]
        Mt = [AR.f32([128, 512]) for _ in range(3)]; Mtb = [NB(), NB(), NB()]
        Ost = [AR.f32([128, n]) for _ in range(2)]; Ostb = [NB(), NB()]
        a, b_ = g0 + lo, g0 + lo + n
        P.dma("sp", Y[:], ysc.rearrange("(k p) t -> p k t", p=128)[:, :, a:b_], key="yl", reads=[ysc_buf],
              writes=[Yb])
        mg = hTt
        for dm in range(KC):
            gs = dm % 2
            P.dma("sp", Gt[gs][:], gsc.rearrange("(r k p) t -> p r k t", p=128, r=3)[:, :, dm, a:b_], key=f"gl{gs}",
                  reads=[gsc_buf], writes=[Gtb[gs]])
            wt01 = W.get(w_branch[l][0:2 * MIXW, dm * 128:(dm + 1) * 128], 16, 128)
            wt2 = W.get(w_branch[l][2 * MIXW:3 * MIXW, dm * 128:(dm + 1) * 128], 8, 128)
            wts = [(wt01[0], wt01[1], 0), (wt01[0], wt01[1], 8), (wt2[0], wt2[1], 0)]
            for (c0, w) in pcs:
                for r in range(3):
                    wt, wb, ko = wts[r]
                    b = next_bank()
                    mm_group(b, w, [(wt[:, ko + k, :], Y[:, r * 8 + k, c0:c0 + w]) for k in range(8)], reads=[Yb, wb])
                    P.op("dve", lambda e, b=b, r=r, c0=c0, w=w, gs=gs: e.tensor_tensor(
                        out=Mt[r][:, 0:w], in0=psb[b][:, 0:w], in1=Gt[gs][:, r, c0:c0 + w], op=ALU.mult),
                        reads=[psbuf[b], Gtb[gs]], writes=[Mtb[r]])
                P.op("dve", lambda e, w=w: e.tensor_tensor(out=Mt[0][:, 0:w], in0=Mt[0][:, 0:w], in1=Mt[1][:, 0:w],
                                                           op=ALU.add), reads=[Mtb[1]], writes=[Mtb[0]], partial=True)
                P.op("dve", lambda e, w=w, c0=c0, dm=dm: e.tensor_tensor(out=mg[:, dm, lo + c0:lo + c0 + w],
                                                                         in0=Mt[0][:, 0:w], in1=Mt[2][:, 0:w], op=ALU.add),
                     reads=[Mtb[0], Mtb[2]], writes=[hT_buf], partial=True)
        for dm in range(KC):
            wt, wb = W.get(w_out[l][:, dm * 128:(dm + 1) * 128], KC, 128)
            os_ = dm % 2
            for (c0, w) in pcs:
                b = next_bank()
                mm_group(b, w, [(wt[:, k, :], mg[:, k, lo + c0:lo + c0 + w]) for k in range(KC)], reads=[hT_buf, wb])
                P.op("act", lambda e, b=b, c0=c0, w=w, os_=os_: e.activation(out=Ost[os_][:, c0:c0 + w],
                                                                           in_=psb[b][:, 0:w], func=AF.Copy),
                     reads=[psbuf[b]], writes=[Ostb[os_]], partial=(c0 > 0))
            P.dma("sp", oT[dm * 128:(dm + 1) * 128, a:b_], Ost[os_][:, :], key=f"os{os_}", reads=[Ostb[os_]],
                  writes=[oT_buf], partial=True)

    def ffn(l, g0, lo, n, first):
        phase_begin()
        pcs = even_split(n, 512)
        gT = AR.bf16([FC, n]); gTb = NB()
        A_ = AR.f32([128, n + 2]); A = [A_, A_]; Ab_ = NB(); Ab = [Ab_, Ab_]
        Tt = [AR.f32([128, n]) for _ in range(2)]; Ttb = [NB(), NB()]
        Ost = Tt; Ostb = Ttb
        a_, b_ = g0 + lo, g0 + lo + n
        wl = w_ffn_in[l]
        if first:
            P.op("dve", lambda e: e.memset(acar[:], 0.0), reads=[acar_buf], writes=[acar_buf])
        for c in range(FC):
            sl = c % 2
            wa, wab = W.get(wl[:, c * 128:(c + 1) * 128], KC, 128)
            wu, wub = W.get(wl[:, DFF + c * 128:DFF + (c + 1) * 128], KC, 128)
            P.op("dve", lambda e, c=c, sl=sl: e.tensor_copy(out=A[sl][:, 0:2], in_=acar[:, c, :]),
                 reads=[acar_buf, gTb], writes=[Ab[sl]])
            for (c0, w) in pcs:
                b = next_bank()
                mm_group(b, w, [(wa[:, k, :], hTt[:, k, lo + c0:lo + c0 + w]) for k in range(KC)], reads=[hT_buf, wab])
                P.op("act", lambda e, b=b, c0=c0, w=w, sl=sl: e.activation(out=A[sl][:, 2 + c0:2 + c0 + w],
                                                                         in_=psb[b][:, 0:w], func=AF.Copy),
                     reads=[psbuf[b]], writes=[Ab[sl]], partial=True)
            P.op("dve", lambda e, c=c, sl=sl: e.tensor_copy(out=acar[:, c, :], in_=A[sl][:, n:n + 2]),
                 reads=[Ab[sl]], writes=[acar_buf], partial=True)
            w0, w1, w2 = [vcol(l, cfg.v_fconv, jj * FC + c) for jj in range(3)]
            P.op("dve", lambda e, sl=sl, w0=w0: e.tensor_scalar(out=Tt[sl][:, :], in0=A[sl][:, 0:n], scalar1=w0,
                                                                scalar2=None, op0=ALU.mult),
                 reads=[Ab[sl], cbuf], writes=[Ttb[sl]])
            P.op("dve", lambda e, sl=sl, w1=w1: e.scalar_tensor_tensor(out=Tt[sl][:, :], in0=A[sl][:, 1:n + 1], scalar=w1,
                                                                       in1=Tt[sl][:, :], op0=ALU.mult, op1=ALU.add),
                 reads=[Ab[sl], cbuf], writes=[Ttb[sl]], partial=True)
            P.op("dve", lambda e, sl=sl, w2=w2: e.scalar_tensor_tensor(out=Tt[sl][:, :], in0=A[sl][:, 2:n + 2], scalar=w2,
                                                                       in1=Tt[sl][:, :], op0=ALU.mult, op1=ALU.add),
                 reads=[Ab[sl], cbuf], writes=[Ttb[sl]], partial=True)
            P.op("act", lambda e, sl=sl: e.activation(out=Tt[sl][:, :], in_=Tt[sl][:, :], func=AF.Gelu_apprx_tanh),
                 reads=[Ttb[sl]], writes=[Ttb[sl]], partial=True)
            for (c0, w) in pcs:
                b = next_bank()
                mm_group(b, w, [(wu[:, k, :], hTt[:, k, lo + c0:lo + c0 + w]) for k in range(KC)], reads=[hT_buf, wub])
                P.op("dve", lambda e, b=b, c0=c0, w=w, sl=sl, c=c: e.tensor_tensor(
                    out=gT[:, c, c0:c0 + w], in0=psb[b][:, 0:w], in1=Tt[sl][:, c0:c0 + w], op=ALU.mult),
                    reads=[psbuf[b], Ttb[sl]], writes=[gTb], partial=True)
        wlo = w_ffn_out[l]
        kparts = [(k0, min(16, FC - k0)) for k0 in range(0, FC, 16)]
        for dm in range(KC):
            wts = [W.get(wlo[k0 * 128:(k0 + kn) * 128, dm * 128:(dm + 1) * 128], kn, 128) for (k0, kn) in kparts]
            os_ = dm % 2
            for (c0, w) in pcs:
                b = next_bank()
                pairs = []
                for (k0, kn), (wt, wb) in zip(kparts, wts):
                    pairs += [(wt[:, k, :], gT[:, k0 + k, c0:c0 + w]) for k in range(kn)]
                mm_group(b, w, pairs, reads=[gTb] + [wb for (_, wb) in wts])
                P.op("act", lambda e, b=b, c0=c0, w=w, os_=os_: e.activation(out=Ost[os_][:, c0:c0 + w],
                                                                           in_=psb[b][:, 0:w], func=AF.Copy),
                     reads=[psbuf[b]], writes=[Ostb[os_]], partial=(c0 > 0))
            P.dma("sp", oT[dm * 128:(dm + 1) * 128, a_:b_], Ost[os_][:, :], key=f"os{os_}", reads=[Ostb[os_]],
                  writes=[oT_buf], partial=True)

    acar_buf = Buf()


    def xchg_x():
        phase_begin()
        XS = AR.f32([KC, 16]); XR = AR.f32([KC, 16]); XP = AR.f32([KC, 16])
        xsb, xrb, xpb = NB(), NB(), NB()
        xv = xT.rearrange("(k p) t -> p k t", p=128)
        P.dma("sp", XS[:], xv[:, :, TS - 16:TS], key="xx0", reads=[xT_buf], writes=[xsb])
        P.dma("sp", send_x.rearrange("(k p) c -> p k c", p=128), XS[:], key="xx1", reads=[xsb], writes=[sendx_buf])
        P.coll(send_x, recv_x, GROUPS, reads=[sendx_buf], writes=[recvx_buf])
        P.dma("sp", XR[:], recv_x[0:D, :].rearrange("(k p) c -> p k c", p=128), key="xx2", reads=[recvx_buf],
              writes=[xrb])
        P.dma("sp", XP[:], xv[:, :, 0:16], key="xx3", reads=[xT_buf], writes=[xpb])
        P.op("dve", lambda e: e.tensor_scalar(out=XP[:], in0=XP[:], scalar1=flag_t[:, 0:1], scalar2=None, op0=ALU.mult),
             reads=[xpb, cbuf], writes=[xpb], partial=True)
        P.op("dve", lambda e: e.scalar_tensor_tensor(out=XP[:], in0=XR[:], scalar=flag_t[:, 1:2], in1=XP[:],
                                                     op0=ALU.mult, op1=ALU.add),
             reads=[xrb, xpb, cbuf], writes=[xpb], partial=True)
        P.dma("sp", xv[:, :, 0:16], XP[:], key="xx4", reads=[xpb], writes=[xT_buf], partial=True)

    def whole():
        transpose_in()
        tiles = [(0, 0, TC // 2), (0, TC // 2, TC - TC // 2)]
        if cfg.stop > 0:
            norm_pass(0, cfg.v_pre_mix, 0, 0, TC)
        for l in range(DEPTH):
            if cfg.stop <= 2 * l:
                break
            if l > 0:
                xchg_x()
                norm_pass(l, cfg.v_pre_mix, 0, 0, N_META)
            mixer_p1(l, 0)
            for (g0, slo, sn) in tiles:
                mixer_p2(l, g0, slo, sn)
                update_pass(l, cfg.v_post_mix, g0, slo, sn, nxt=(l, cfg.v_pre_ffn))
            if cfg.stop <= 2 * l + 1:
                break
            xchg_x()
            norm_pass(l, cfg.v_pre_ffn, 0, 0, N_META)
            for ti, (g0, slo, sn) in enumerate(tiles):
                ffn(l, g0, slo, sn, first=(ti == 0))
                update_pass(l, cfg.v_post_ffn, g0, slo, sn,
                            nxt=(l + 1, cfg.v_pre_mix) if (l + 1 < DEPTH and cfg.stop > 2 * l + 2) else None)
        transpose_out()

    PERS = psbuf + [hT_buf, xT_buf, oT_buf, ysc_buf, gsc_buf, acar_buf, car_buf, cbuf, gd_buf, ksc_buf, cfin_buf, sendc_buf, recvc_buf, sendx_buf, recvx_buf] + W.slots
    snap = [(dict(b.w.d), dict(b.r.d)) for b in PERS]
    saved = (P.streams, P.dma_cnt, ring["i"], guard["tok"])
    P.dry = True
    P.streams = {e: [] for e in ENGS}
    P.dma_cnt = {}
    whole()
    P.streams, P.dma_cnt, ring["i"], guard["tok"] = saved
    P.dry = False
    P.ncoll = 0
    W.i = 0
    del LIVE[:]
    for b, (w_, r_) in zip(PERS, snap):
        b.w.d = dict(w_)
        b.r.d = dict(r_)
    whole()
    P.emit()
    return nc, es, P


def host_vecs(cfg, inp):
    KC, FC = cfg.KC, cfg.FC
    v = np.zeros((128, cfg.DEPTH * cfg.NV), np.float32)
    for l in range(cfg.DEPTH):
        o = l * cfg.NV

        def put(off, arr, n):
            v[:, o + off:o + off + n] = np.asarray(arr, np.float32).reshape(n, 128).T

        put(cfg.v_pre_mix, inp["norm_pre_mix"][l], KC)
        put(cfg.v_post_mix, inp["norm_post_mix"][l], KC)
        put(cfg.v_pre_ffn, inp["norm_pre_ffn"][l], KC)
        put(cfg.v_post_ffn, inp["norm_post_ffn"][l], KC)
        put(cfg.v_gain, inp["mlstm_head_gain"][l], 8)
        put(cfg.v_convw, inp["conv_mix_w"][l].reshape(-1), 24)
        put(cfg.v_pscale, inp["pool_scale"][l], 8)
        put(cfg.v_fconv, inp["ffn_conv_w"][l].reshape(-1), 3 * FC)
    bif = np.zeros((64, cfg.DEPTH * 16), np.float32)
    for l in range(cfg.DEPTH):
        bif[:, l * 16:l * 16 + 16] = np.asarray(inp["b_if"][l], np.float32).reshape(1, 16)
    invc = np.zeros((128, 64), np.float32)
    for g, w in enumerate(POOL_W):
        invc[:, g * 16:(g + 1) * 16] = (1.0 / np.minimum(np.arange(16) + 1, w)).astype(np.float32)[None, :]
    return v, bif, invc


_CACHE = {}


def run(cfg, inp):
    key = (cfg.D, cfg.DFF, cfg.NT, cfg.NSUP, cfg.DEPTH, cfg.stop, cfg.n_cores)
    if key not in _CACHE:
        _CACHE[key] = build_program(cfg)
    nc, es, P = _CACHE[key]
    B = inp["x"].shape[0]
    n_cores = cfg.n_cores
    assert n_cores == 2 * B
    NT = cfg.NT
    v, bif, invc = host_vecs(cfg, inp)
    f = lambda a: np.ascontiguousarray(np.asarray(a, np.float32))
    common = {
        "vecs": v, "bif": bif, "invc": invc,
        "w_in": f(inp["w_in"]),
        "pool_w": f(inp["pool_w"]).reshape(cfg.DEPTH, 1024, 256),
        "w_branch": f(inp["w_branch"]).reshape(cfg.DEPTH, 3 * MIXW, cfg.D),
        "w_out": f(inp["w_out"]),
        "w_ffn_in": f(inp["w_ffn_in"]),
        "w_ffn_out": f(inp["w_ffn_out"]),
    }
    meta = f(inp["meta_tokens"])
    x = f(inp["x"])
    in_maps = []
    for c in range(n_cores):
        b, h = c // 2, c % 2
        m = dict(common)
        pre = meta if h == 0 else x[b, NT - N_META:NT]
        m["xin"] = np.ascontiguousarray(np.concatenate([pre, x[b, h * NT:(h + 1) * NT]], axis=0))
        fl = np.zeros((128, 2), np.float32)
        fl[:, h] = 1.0
        m["flags"] = fl
        in_maps.append(m)
    res = run_bass_kernel_spmd(nc, in_maps, core_ids=list(range(n_cores)))
    if getattr(cfg, "debug", False):
        cfg.dbg = res.results
    outs = [np.asarray(res.results[c]["out"], np.float32) for c in range(n_cores)]
    return np.stack([np.concatenate([outs[2 * b], outs[2 * b + 1]], axis=0) for b in range(B)], axis=0)


def kernel(**inputs):
    cfg = Cfg()
    return run(cfg, inputs)
```
